# Optimizing a Trainium2 kernel written in Bass

```python
import jax, jax.numpy as jnp
from jax import lax
import numpy as np

D_MODEL = 1024
BATCH = 8
SEQ = 4096
DEPTH = 4

N_A_LAYERS = DEPTH // 2
N_B_LAYERS = DEPTH - N_A_LAYERS
EPS = 1e-6

CHUNK = 128
A_WIDTH = 2 * D_MODEL
A_GROUP_WIDTH = 256
A_GROUPS = A_WIDTH // A_GROUP_WIDTH

HEAD_DIM = 128
DILATED_GROUPS = ((128, 1), (512, 4), (2048, 16))
N_GROUPS = len(DILATED_GROUPS)
Q_HEADS_PER_GROUP = D_MODEL // HEAD_DIM
KV_HEADS_PER_GROUP = 2
Q_PER_KV = Q_HEADS_PER_GROUP // KV_HEADS_PER_GROUP
N_Q_HEADS = N_GROUPS * Q_HEADS_PER_GROUP
N_KV_HEADS = N_GROUPS * KV_HEADS_PER_GROUP
B_WIDTH = Q_HEADS_PER_GROUP * HEAD_DIM
BAND = 128

kernel_name = "yoco_gmlp_dilated_hybrid"


def rms_norm(x, g):
    xf = x.astype(jnp.float32)
    y = xf * lax.rsqrt(jnp.mean(xf * xf, axis=-1, keepdims=True) + EPS)
    return (y * g.astype(jnp.float32)).astype(x.dtype)


def ada_mod(c, w, b, n):
    h = jax.nn.silu(c) @ w + b
    return jnp.split(h[:, None, :], n, axis=-1)


def alibi_slopes():
    h = jnp.arange(1, N_Q_HEADS + 1, dtype=jnp.float32)
    return jnp.exp2(-8.0 * h / N_Q_HEADS)


def to_dilated(t, d):
    b, s = t.shape[:2]
    rest = t.shape[2:]
    span = d * BAND
    sp = -(-s // span) * span
    t = jnp.pad(t, [(0, 0), (0, sp - s)] + [(0, 0)] * len(rest))
    t = t.reshape((b, sp // d, d) + rest)
    t = jnp.swapaxes(t, 1, 2)
    return t.reshape((b, d, sp // span, BAND) + rest)


def from_dilated(t, s):
    b, d, nb = t.shape[:3]
    rest = t.shape[4:]
    t = t.reshape((b, d, nb * BAND) + rest)
    t = jnp.swapaxes(t, 1, 2)
    return t.reshape((b, nb * BAND * d) + rest)[:, :s]


def band_keys(t):
    prev = jnp.pad(t[:, :, :-1], [(0, 0), (0, 0), (1, 0)] + [(0, 0)] * 3)
    return jnp.concatenate([prev, t], axis=3)


def gmlp_layer(x, c, ada_w, ada_b, norm_g, w_in, sgu_g, w_s, b_s, w_out):
    bsz, s, _ = x.shape
    shift, scale, gate = ada_mod(c, ada_w, ada_b, 3)
    h = rms_norm(x, norm_g) * (1 + scale) + shift
    u, v, z = jnp.split(h @ w_in, 3, axis=-1)
    u = jax.nn.gelu(u)
    v = rms_norm(jax.nn.gelu(v), sgu_g)
    v = v.reshape(bsz, s // CHUNK, CHUNK, A_GROUPS, A_GROUP_WIDTH)
    w_causal = jnp.tril(w_s)
    mixed = jnp.einsum('gts,bcsge->bctge', w_causal, v) + b_s.T[None, None, :, :, None]
    y = u * mixed.reshape(bsz, s, A_WIDTH) * jax.nn.silu(z)
    return x + gate * (y @ w_out)


def shared_kv(x, c, ada_w, ada_b, norm_g, w_kv, k_norm_g):
    bsz, s, _ = x.shape
    shift, scale = ada_mod(c, ada_w, ada_b, 2)
    h = rms_norm(x, norm_g) * (1 + scale) + shift
    k, v = jnp.split(h @ w_kv, 2, axis=-1)
    k = rms_norm(k.reshape(bsz, s, N_KV_HEADS, HEAD_DIM), k_norm_g)
    v = v.reshape(bsz, s, N_KV_HEADS, HEAD_DIM)
    groups = []
    for g, (window, d) in enumerate(DILATED_GROUPS):
        hs = slice(g * KV_HEADS_PER_GROUP, (g + 1) * KV_HEADS_PER_GROUP)
        groups.append((band_keys(to_dilated(k[:, :, hs], d)),
                       band_keys(to_dilated(v[:, :, hs], d))))
    return groups


def dilated_group_attn(q, kb, vb, slopes, window, d):
    s = q.shape[1]
    ql = to_dilated(q, d)
    bsz, _, nb = ql.shape[:3]
    ql = ql.reshape(bsz, d, nb, BAND, KV_HEADS_PER_GROUP, Q_PER_KV, HEAD_DIM)
    sc = jnp.einsum('brnikgh,brnjkh->brnkgij', ql, kb, preferred_element_type=jnp.float32)
    i = jnp.arange(BAND)[:, None]
    j = jnp.arange(2 * BAND)[None, :]
    dq = BAND + i - j
    key_idx = (jnp.arange(nb)[:, None, None] - 1) * BAND + j
    valid = (dq >= 0) & (dq <= window // d) & (key_idx >= 0)
    bias = -slopes.astype(jnp.float32).reshape(KV_HEADS_PER_GROUP, Q_PER_KV)[:, :, None, None] \
        * (d * dq).astype(jnp.float32)
    sc = jnp.where(valid[:, None, None], sc + bias, -jnp.inf)
    m = jnp.max(sc, axis=-1, keepdims=True)
    p = jnp.exp(sc - m)
    l = jnp.sum(p, axis=-1, keepdims=True)
    o = jnp.einsum('brnkgij,brnjkh->brnikgh', p / l, vb.astype(jnp.float32))
    lse = jnp.moveaxis((m + jnp.log(l))[..., 0], -1, 3)
    o = from_dilated(o.reshape(bsz, d, nb, BAND, Q_HEADS_PER_GROUP, HEAD_DIM), s)
    lse = from_dilated(lse.reshape(bsz, d, nb, BAND, Q_HEADS_PER_GROUP), s)
    return o, lse


def dilated_layer(x, c, kv_groups, ada_w, ada_b, norm_g, w_in, q_norm_g, w_out):
    bsz, s, _ = x.shape
    shift, scale, gate = ada_mod(c, ada_w, ada_b, 3)
    h = rms_norm(x, norm_g) * (1 + scale) + shift
    qz = h @ w_in
    q = qz[..., :N_Q_HEADS * HEAD_DIM].reshape(bsz, s, N_Q_HEADS, HEAD_DIM)
    z = qz[..., N_Q_HEADS * HEAD_DIM:]
    q = rms_norm(q, q_norm_g) * (HEAD_DIM ** -0.5)
    slopes = alibi_slopes()
    outs, lses = [], []
    for g, (window, d) in enumerate(DILATED_GROUPS):
        hs = slice(g * Q_HEADS_PER_GROUP, (g + 1) * Q_HEADS_PER_GROUP)
        kb, vb = kv_groups[g]
        o, lse = dilated_group_attn(q[:, :, hs], kb, vb, slopes[hs], window, d)
        outs.append(o)
        lses.append(lse)
    alpha = jax.nn.softmax(jnp.stack(lses), axis=0)
    o = jnp.sum(alpha[..., None] * jnp.stack(outs), axis=0).astype(x.dtype)
    y = o.reshape(bsz, s, B_WIDTH) * jax.nn.silu(z)
    return x + gate * (y @ w_out)


def setup_inputs(seed: int = 0) -> dict:
    key = jax.random.key(seed)
    ks = jax.random.split(key, 24)
    nrm = lambda k, shape, sc: jax.random.normal(k, shape, jnp.float32) * sc
    D = D_MODEL
    return {
        "x": nrm(ks[0], (BATCH, SEQ, D), 1.0),
        "c": nrm(ks[1], (BATCH, D), 1.0),
        "a_ada_w": nrm(ks[2], (N_A_LAYERS, D, 3 * D), 0.5 * D ** -0.5),
        "a_ada_b": nrm(ks[3], (N_A_LAYERS, 3 * D), 0.01),
        "a_norm_g": 1.0 + nrm(ks[4], (N_A_LAYERS, D), 0.05),
        "a_w_in": nrm(ks[5], (N_A_LAYERS, D, 3 * A_WIDTH), D ** -0.5),
        "a_sgu_g": 1.0 + nrm(ks[6], (N_A_LAYERS, A_WIDTH), 0.05),
        "a_w_spatial": nrm(ks[7], (N_A_LAYERS, A_GROUPS, CHUNK, CHUNK), 0.5 * CHUNK ** -0.5),
        "a_b_spatial": 1.0 + nrm(ks[8], (N_A_LAYERS, A_GROUPS, CHUNK), 0.01),
        "a_w_out": nrm(ks[9], (N_A_LAYERS, A_WIDTH, D), A_WIDTH ** -0.5),
        "kv_ada_w": nrm(ks[10], (D, 2 * D), 0.5 * D ** -0.5),
        "kv_ada_b": nrm(ks[11], (2 * D,), 0.01),
        "kv_norm_g": 1.0 + nrm(ks[12], (D,), 0.05),
        "w_kv": nrm(ks[13], (D, 2 * N_KV_HEADS * HEAD_DIM), D ** -0.5),
        "k_norm_g": 1.0 + nrm(ks[14], (HEAD_DIM,), 0.05),
        "b_ada_w": nrm(ks[15], (N_B_LAYERS, D, 3 * D), 0.5 * D ** -0.5),
        "b_ada_b": nrm(ks[16], (N_B_LAYERS, 3 * D), 0.01),
        "b_norm_g": 1.0 + nrm(ks[17], (N_B_LAYERS, D), 0.05),
        "b_w_in": nrm(ks[18], (N_B_LAYERS, D, N_Q_HEADS * HEAD_DIM + B_WIDTH), D ** -0.5),
        "b_q_norm_g": 1.0 + nrm(ks[19], (N_B_LAYERS, HEAD_DIM), 0.05),
        "b_w_out": nrm(ks[20], (N_B_LAYERS, B_WIDTH, D), B_WIDTH ** -0.5),
    }


def reference(x, c, a_ada_w, a_ada_b, a_norm_g, a_w_in, a_sgu_g, a_w_spatial, a_b_spatial,
              a_w_out, kv_ada_w, kv_ada_b, kv_norm_g, w_kv, k_norm_g, b_ada_w, b_ada_b,
              b_norm_g, b_w_in, b_q_norm_g, b_w_out):
    kv_groups = None
    for layer in range(DEPTH):
        if layer < N_A_LAYERS:
            x = gmlp_layer(x, c, a_ada_w[layer], a_ada_b[layer], a_norm_g[layer], a_w_in[layer],
                           a_sgu_g[layer], a_w_spatial[layer], a_b_spatial[layer], a_w_out[layer])
        else:
            if layer == N_A_LAYERS:
                kv_groups = shared_kv(x, c, kv_ada_w, kv_ada_b, kv_norm_g, w_kv, k_norm_g)
            lb = layer - N_A_LAYERS
            x = dilated_layer(x, c, kv_groups, b_ada_w[lb], b_ada_b[lb], b_norm_g[lb], b_w_in[lb],
                              b_q_norm_g[lb], b_w_out[lb])
    return x
```

```python
import os as _os
import numpy as np
from contextlib import ExitStack
import concourse.bass as bass
import concourse.mybir as mybir
from concourse.bass_utils import run_bass_kernel_spmd

F32 = mybir.dt.float32
BF16 = mybir.dt.bfloat16
AF = mybir.ActivationFunctionType
ALU = mybir.AluOpType

S = 4096
D = 1024
NT = S // 128
EPS = 1e-6
ARENA_BYTES = 211968
EXP_SHIFT = 4.0
PE_FIX = float(_os.environ.get("MK_PE_FIX", "0.1"))
QRAW = int(_os.environ.get("MK_QRAW", "0"))
STRICT = int(_os.environ.get("MK_STRICT", "2"))
DMA_BW = float(_os.environ.get("MK_DMA_BW", "450e3"))
PRIO_MODE = int(_os.environ.get("MK_PRIO", "1"))
PRIO_W = float(_os.environ.get("MK_PRIO_W", "1000.0"))
SEM_LAT = float(_os.environ.get("MK_LAT", "0.8"))
COMPUTE = ("pe", "act", "dve", "pool")
ISSUERS = ("pe", "act", "dve", "pool", "sp")


class _CostProbe:
    def __init__(self, issuer):
        self.issuer = issuer
        self.t = 0.0
        self.aset = None

    @staticmethod
    def _free(ap):
        n = 1
        for s_ in ap.shape[1:]:
            n *= s_
        return n

    def matmul(self, out, lhsT=None, rhs=None, **kw):
        c = self._free(rhs)
        if c >= 512:
            n = 0.216 * c / 512.0
        elif c >= 256:
            n = 0.110 + (c - 256) * (0.216 - 0.110) / 256.0
        elif c >= 128:
            n = 0.056 + (c - 128) * (0.110 - 0.056) / 128.0
        else:
            n = 0.030 + max(c - 64, 0) * (0.056 - 0.030) / 64.0
        if self.t == 0.0:
            n += PE_FIX
        if rhs.dtype == F32:
            n *= 4
        self.t += n
        return self

    def transpose(self, out, in_, ident):
        self.t += (0.25 if in_.dtype == F32 else 0.06) + (PE_FIX if self.t == 0.0 else 0.0)
        return self

    def activation(self, out=None, in_=None, func=None, **kw):
        self.t += self._free(in_) / 1100.0 + 0.15
        if func in (AF.Gelu_apprx_tanh, AF.Tanh):
            self.aset = "g"
        elif func == AF.Silu:
            self.aset = "s"
        elif func in (AF.Ln, AF.Exp):
            self.aset = "le"
        return self

    def _ew(self, out):
        if self.issuer == "pool":
            self.t += self._free(out) / 450.0 + 0.2
        else:
            self.t += self._free(out) / 800.0 + 0.12
        return self

    def tensor_tensor(self, out=None, **kw):
        return self._ew(out)

    def tensor_scalar(self, out=None, **kw):
        return self._ew(out)

    def scalar_tensor_tensor(self, out=None, **kw):
        return self._ew(out)

    def tensor_copy(self, out=None, **kw):
        return self._ew(out)

    def memset(self, ap, c):
        return self._ew(ap)

    def dma_start(self, out=None, in_=None, **kw):
        esz = 4 if (out.dtype == F32 or in_.dtype == F32) else 2
        n = 1
        for s_ in out.shape:
            n *= s_
        self.t += n * esz / DMA_BW
        return self


class Sched:
    def __init__(self):
        self.ops = []
        self.last_w = {}
        self.readers = {}
        self.chan_last = {}
        self.seg = 0
        self.fixed_segs = set()

    def add(self, issuer, fn, reads=(), writes=(), chan=None, extra_deps=(), est=None, aset=None):
        idx = len(self.ops)
        if est is None:
            if fn is None:
                est = 0.0
            else:
                pr = _CostProbe(issuer)
                fn(pr)
                est = pr.t
                aset = pr.aset
        key = issuer if chan is None else ("dma", chan)
        deps = {}
        for r in reads:
            w = self.last_w.get(r)
            if w is not None:
                deps[w] = True
        for r in writes:
            w = self.last_w.get(r)
            if w is not None:
                deps.setdefault(w, False)
            for ri in self.readers.get(r, ()):
                deps.setdefault(ri, False)
        for j in extra_deps:
            deps[j] = True
        deps.pop(idx, None)
        self.ops.append(dict(idx=idx, issuer=issuer, key=key, fn=fn, deps=deps, flag=chan is not None,
                             est=est, aset=aset, seg=self.seg, pos=0, count=0, waits=[], dma=chan is not None,
                             tag=(list(writes) or ["-"])[0]))
        for r in reads:
            self.readers.setdefault(r, []).append(idx)
        for r in writes:
            self.last_w[r] = idx
            self.readers[r] = []
        if chan is not None:
            self.chan_last[chan] = idx
        return idx

    def barrier(self, marker_fns, keep=(), skip_chans=()):
        self.seg += 1
        self.fixed_segs.add(self.seg)
        marks = []
        for e in COMPUTE:
            marks.append(self.add(e, marker_fns[e], reads=marker_fns.get(e + "_r", ()), writes=marker_fns[e + "_w"], est=0.1))
        deps = marks + [v for c, v in self.chan_last.items() if c not in skip_chans]
        for e in ISSUERS:
            self.add(e, None, extra_deps=deps, est=0.0)
        self.seg += 1
        kept = {}
        for r, w in self.last_w.items():
            if isinstance(r, tuple) and r[0] in keep:
                kept[r] = w
        self.last_w = kept
        self.readers = {}

    def finish(self):
        self.seg += 1
        self.fixed_segs.add(self.seg)
        self.add("sp", None, extra_deps=list(self.chan_last.values()), est=0.0)

    def schedule(self):
        ops = self.ops
        n = len(ops)
        end = [0.0] * n
        startt = [0.0] * n
        free = {e: 0.0 for e in ISSUERS}
        act_set = [None]
        pipe = [0.0]
        order = []
        LAT = SEM_LAT

        def ready_time(op):
            t = 0.0
            for j in op["deps"]:
                pj = ops[j]
                lat = 0.0 if (pj["key"] == op["key"] and not pj["dma"]) else LAT
                if end[j] + lat > t:
                    t = end[j] + lat
            return t

        def place(op, st):
            i = op["idx"]
            dur = op["est"]
            if op["issuer"] == "act" and op["aset"] is not None and op["aset"] != act_set[0]:
                dur += 1.3
                act_set[0] = op["aset"]
            startt[i] = st
            if op["dma"]:
                issue = 1.0 if op["issuer"] == "pool" else 0.08
                free[op["issuer"]] = st + issue
                xs = max(st + issue + 1.5, pipe[0])
                end[i] = xs + dur + 0.5
                pipe[0] = xs + dur
            else:
                free[op["issuer"]] = st + dur
                end[i] = st + dur
            order.append(i)

        segs = {}
        for op in ops:
            segs.setdefault(op["seg"], []).append(op["idx"])
        for sg in sorted(segs):
            idxs = segs[sg]
            if sg in self.fixed_segs:
                for i in idxs:
                    op = ops[i]
                    place(op, max(free[op["issuer"]], ready_time(op)))
                continue
            inseg = set(idxs)
            indeg = {}
            succ = {}
            for i in idxs:
                c = 0
                for j in ops[i]["deps"]:
                    if j in inseg:
                        c += 1
                        succ.setdefault(j, []).append(i)
                indeg[i] = c
            prio = {}
            if PRIO_MODE:
                bl = {}
                for i in reversed(idxs):
                    m_ = 0.0
                    for k in succ.get(i, ()):
                        v = bl[k] + LAT
                        if v > m_:
                            m_ = v
                    bl[i] = ops[i]["est"] + m_
                tot = max(bl.values()) if bl else 1.0
                for i in idxs:
                    prio[i] = i - PRIO_W * bl[i]
            else:
                for i in idxs:
                    prio[i] = i
            ready = {e: [] for e in ISSUERS}
            rt = {}
            for i in idxs:
                if indeg[i] == 0:
                    rt[i] = ready_time(ops[i])
                    ready[ops[i]["issuer"]].append(i)
            left = len(idxs)
            while left:
                best = None
                for e in ISSUERS:
                    rl = ready[e]
                    if not rl:
                        continue
                    now = free[e]
                    avail = [i for i in rl if rt[i] <= now]
                    if avail:
                        pick = min(avail, key=lambda i: prio[i])
                        if e == "act" and rt[pick] > now - 4.0:
                            same = [i for i in avail if ops[i]["aset"] in (None, act_set[0])]
                            if same:
                                pick = min(same, key=lambda i: prio[i])
                        st = now
                    else:
                        pick = min(rl, key=lambda i: (rt[i], prio[i]))
                        st = rt[pick]
                    if best is None or (st, pick) < best[:2]:
                        best = (st, pick, e)
                st, pick, e = best
                ready[e].remove(pick)
                place(ops[pick], st)
                left -= 1
                for k in succ.get(pick, ()):
                    indeg[k] -= 1
                    if indeg[k] == 0:
                        rt[k] = ready_time(ops[k])
                        ready[ops[k]["issuer"]].append(k)
        self.order = order
        self.sim_endt = end
        self.sim_end = max(end) if end else 0.0
        self.sim_start = startt
        return self.sim_end

    def report(self):
        segs = {}
        for op in self.ops:
            i = op["idx"]
            d = segs.setdefault(op["seg"], dict(t0=1e18, t1=0.0, busy={e: 0.0 for e in ISSUERS}, n=0))
            d["t0"] = min(d["t0"], self.sim_start[i])
            if not op["dma"]:
                d["t1"] = max(d["t1"], self.sim_endt[i])
                d["busy"][op["issuer"]] += op["est"]
            d["n"] += 1
        for sg in sorted(segs):
            d = segs[sg]
            if sg in self.fixed_segs:
                continue
            dur = d["t1"] - d["t0"]
            print("seg %3d t0=%8.1f dur=%8.1f n=%5d busy%%: %s" % (
                sg, d["t0"], dur, d["n"],
                " ".join("%s=%3.0f" % (e, 100 * d["busy"][e] / max(dur, 1e-9)) for e in COMPUTE)), flush=True)

    def resolve(self):
        ops = self.ops
        self.schedule()
        pos = {e: 0 for e in ISSUERS}
        for i in self.order:
            op = ops[i]
            op["pos"] = pos[op["issuer"]]
            pos[op["issuer"]] += 1
        need = {}
        for i in self.order:
            op = ops[i]
            lst = []
            for j, raw in op["deps"].items():
                pj = ops[j]
                if pj["fn"] is None:
                    continue
                if pj["key"] == op["key"] and pj["key"] in COMPUTE:
                    if STRICT == 0 and (not raw or op["pos"] - pj["pos"] >= 4):
                        continue
                    if STRICT == 1 and not raw and op["pos"] - pj["pos"] >= 64:
                        continue
                pj["flag"] = True
                lst.append(j)
            need[i] = lst
        counts = {}
        for i in self.order:
            op = ops[i]
            if op["flag"] and op["fn"] is not None:
                k = op["key"]
                counts[k] = counts.get(k, 0) + (16 if isinstance(k, tuple) else 1)
                op["count"] = counts[k]
        waited = {e: {} for e in ISSUERS}
        for i in self.order:
            op = ops[i]
            w = {}
            for j in need[i]:
                pj = ops[j]
                w[pj["key"]] = max(w.get(pj["key"], 0), pj["count"])
            wd = waited[op["issuer"]]
            for k, c in w.items():
                if wd.get(k, 0) < c:
                    wd[k] = c
                    op["waits"].append((k, c))
        self.keys = list(counts.keys())

    def emit(self, issuer, eng, sems):
        for i in self.order:
            op = self.ops[i]
            if op["issuer"] != issuer:
                continue
            for k, c in op["waits"]:
                eng.wait_ge(sems[k], c)
            if op["fn"] is None:
                continue
            ins = op["fn"](eng)
            if op["flag"]:
                ins.then_inc(sems[op["key"]], 16 if isinstance(op["key"], tuple) else 1)


class Arena:
    def __init__(self, ap):
        self.ap = ap
        self.off = 0

    def reset(self, off=0):
        self.off = off

    def take(self, dtype, shape):
        n = 1
        for s_ in shape[1:]:
            n *= s_
        esz = 4 if dtype == F32 else 2
        nbytes = (n * esz + 63) // 64 * 64
        assert self.off + nbytes <= ARENA_BYTES, (self.off, nbytes)
        o32 = self.off // 4
        v = self.ap[:, o32:o32 + nbytes // 4]
        self.off += nbytes
        if dtype != F32:
            v = v.bitcast(dtype)
        v = v[:, 0:n]
        if len(shape) == 3:
            v = v.rearrange("p (a b) -> p a b", b=shape[2])
        elif len(shape) == 4:
            v = v.rearrange("p (a b c) -> p a b c", b=shape[2], c=shape[3])
        return v


def build_program(n_a=2, do_kv=True, n_b=2):
    nc = bass.Bass("TRN2", target_bir_lowering=False)
    sch = Sched()

    def din(name, shape):
        return nc.dram_tensor(name, list(shape), F32, kind="ExternalInput").ap()

    x_d = din("x", [S, D])
    c_d = din("cT", [128, 8])
    ident_d = din("ident", [128, 128])
    triu_d = din("triu", [128, 128])
    etab_d = din("etab", [128, 12, 512])
    a_ada_w = din("a_ada_w", [2, D, 3072])
    a_ada_b = din("a_ada_b_bc", [2, 128, 3072])
    a_norm_g = din("a_norm_g_pp", [2, 128, 8])
    a_w_in = din("a_w_in", [2, D, 6144])
    a_sgu = din("a_sgu_bc", [2, 128, 2048])
    a_wsT = din("a_wsT", [2, 128, 8, 128])
    a_bsb = din("a_bsb", [2, 128, 8, 128])
    a_w_out = din("a_w_out", [2, 2048, D])
    kv_ada_w = din("kv_ada_w", [D, 2048])
    kv_ada_b = din("kv_ada_b_bc", [128, 2048])
    kv_norm_g = din("kv_norm_g_pp", [128, 8])
    w_kv = din("w_kv", [D, 1536])
    k_norm_g = din("k_norm_g_pp", [128, 1])
    b_ada_w = din("b_ada_w", [2, D, 3072])
    b_ada_b = din("b_ada_b_bc", [2, 128, 3072])
    b_norm_g = din("b_norm_g_pp", [2, 128, 8])
    b_w_in = din("b_w_in", [2, D, 4096])
    b_qg = din("b_q_norm_g_pp", [2, 128, 1])
    b_w_out = din("b_w_out", [2, D, D])
    out_d = nc.dram_tensor("out", [S, D], F32, kind="ExternalOutput").ap()
    qs_d = nc.dram_tensor("qs", [24, 16, 128, 256], BF16).ap()
    szs_d = nc.dram_tensor("szs", [16, 128, 8, 256], BF16).ap()
    gate_d = nc.dram_tensor("gate_sc", [128, 1024], F32).ap()

    es = ExitStack()
    arena_t = es.enter_context(nc.sbuf_tensor("arena", [128, ARENA_BYTES // 4], F32))
    psum_t = es.enter_context(nc.psum_tensor("psum", [128, 4096], F32))
    ar = Arena(arena_t[:])
    ps = psum_t[:]

    def bank(b):
        return ps[:, 512 * b:512 * (b + 1)]

    def bank_bf(b):
        return bank(b).bitcast(BF16)

    ident_bf = ar.take(BF16, [128, 128])
    ident_f = ar.take(F32, [128, 128])
    ones_bf = ar.take(BF16, [128, 128])
    triu_f = ar.take(F32, [128, 128])
    c_col = ar.take(F32, [128, 8])
    s_col = ar.take(F32, [128, 8])
    gs_pp = ar.take(F32, [128, 8])
    sh_pp = ar.take(F32, [128, 8])
    ng_pp = ar.take(F32, [128, 8])
    qg_pp = ar.take(F32, [128, 1])
    kg_pp = ar.take(F32, [128, 1])
    stat = ar.take(F32, [128, 64])
    scr = ar.take(F32, [128, 16])
    PERSIST_END = ar.off

    statn = [0]
    uch = [0]

    def uchan():
        uch[0] += 1
        return "u%d" % uch[0]

    def stat_slot():
        i = statn[0] % 64
        statn[0] += 1
        return stat[:, i:i + 1], ("stat", i)

    def mk_markers():
        return {
            "pe": lambda e: e.matmul(bank(7)[:, 0:2], lhsT=ident_bf[:, 0:128], rhs=ident_bf[:, 0:2],
                                     start=True, stop=True),
            "pe_w": [("ps", 7)],
            "pe_r": ["ident_bf"],
            "act_r": ["scr_a"],
            "act": lambda e: e.activation(out=scr[:, 0:1], in_=scr[:, 1:2], func=AF.Copy),
            "act_w": ["scr_a"],
            "dve": lambda e: e.memset(scr[:, 2:3], 0.0),
            "dve_w": ["scr_d"],
            "pool": lambda e: e.memset(scr[:, 4:5], 0.0),
            "pool_w": ["scr_p"],
        }

    def barrier(keep=(), skip_chans=()):
        sch.barrier(mk_markers(), keep=keep, skip_chans=skip_chans)
        uch[0] = 0

    sch.add("pool", lambda e: e.dma_start(out=ident_bf, in_=ident_d[:, :]), writes=["ident_bf"], chan="pconst")
    sch.add("sp", lambda e: e.dma_start(out=ident_f, in_=ident_d[:, :]), writes=["ident_f"], chan=uchan())
    sch.add("sp", lambda e: e.dma_start(out=triu_f, in_=triu_d[:, :]), writes=["triu_f"], chan=uchan())
    sch.add("sp", lambda e: e.dma_start(out=c_col, in_=c_d[:, :]), writes=["c_col"], chan=uchan())
    sch.add("dve", lambda e: e.memset(ones_bf, 1.0), writes=["ones_bf"])
    sch.add("dve", lambda e: e.memset(scr, 0.0), writes=["scr_a", "scr_d", "scr_p"])
    sch.add("act", lambda e: e.activation(out=s_col, in_=c_col, func=AF.Silu), reads=["c_col"], writes=["s_col"])

    def ada_phase(w_ap, b_ap, ncols, work_off, cw=512):
        ar.reset(work_off)
        ada_bc = ar.take(F32, [128, 3072])
        s_rep = ar.take(F32, [128, 8, 128])
        awb = [ar.take(F32, [128, 8, cw]) for _ in range(2)]
        for kc in range(8):
            sch.add("dve", lambda e, kc=kc: e.tensor_copy(out=s_rep[:, kc, :],
                                                           in_=s_col[:, kc:kc + 1].to_broadcast([128, 128])),
                    reads=["s_col"], writes=["s_rep"])
        wv = w_ap.rearrange("(c p) n -> p c n", p=128)
        nch = ncols // cw
        sch.add("sp", lambda e: e.dma_start(out=ada_bc[:, 0:ncols], in_=b_ap), writes=["ada_bc"], chan=uchan())
        for j in range(nch):
            b = j % 2
            for h in range(2):
                sch.add("sp", lambda e, j=j, b=b, h=h: e.dma_start(
                    out=awb[b][:, 4 * h:4 * h + 4, :], in_=wv[:, 4 * h:4 * h + 4, cw * j:cw * (j + 1)]),
                    writes=[("aw", b, h)], chan=("aw", b, h))
            pb = 4 + (j % 2)

            def mm(e, j=j, b=b, pb=pb):
                for kc in range(8):
                    ins = e.matmul(bank(pb)[:, 0:cw], lhsT=s_rep[:, kc, :], rhs=awb[b][:, kc, :],
                                   start=(kc == 0), stop=(kc == 7))
                return ins
            sch.add("pe", mm, reads=[("aw", b, 0), ("aw", b, 1), "s_rep"], writes=[("ps", pb)])
            sch.add("dve", lambda e, j=j, pb=pb: e.tensor_tensor(
                out=ada_bc[:, cw * j:cw * (j + 1)], in0=bank(pb)[:, 0:cw], in1=ada_bc[:, cw * j:cw * (j + 1)],
                op=ALU.add), reads=[("ps", pb), "ada_bc"], writes=["ada_bc"])
        return ada_bc

    def ada_pp(norm_ap, ada_bc):
        sch.add("sp", lambda e: e.dma_start(out=ng_pp, in_=norm_ap), writes=["ng_pp"], chan=uchan())
        for half in range(2):
            for q in range(2):
                pb = 4 + q

                def tr(e, half=half, q=q, pb=pb):
                    for i in range(4):
                        col = 1024 * half + 128 * (4 * q + i)
                        ins = e.transpose(bank(pb)[:, 128 * i:128 * (i + 1)], ada_bc[:, col:col + 128], ident_f)
                    return ins
                sch.add("pe", tr, reads=["ada_bc", "ident_f"], writes=[("ps", pb)])
                dst = (sh_pp if half == 0 else gs_pp)[:, 4 * q:4 * q + 4]
                src = bank(pb).rearrange("p (a b) -> p a b", b=128)[:, :, 0]
                sch.add("dve", lambda e, dst=dst, src=src: e.tensor_copy(out=dst, in_=src),
                        reads=[("ps", pb)], writes=["sh_pp" if half == 0 else "gs_pp"])
        sch.add("dve", lambda e: e.scalar_tensor_tensor(out=gs_pp, in0=gs_pp, scalar=1.0, in1=ng_pp,
                                                         op0=ALU.add, op1=ALU.mult),
                reads=["gs_pp", "ng_pp"], writes=["gs_pp"])

    def front(src_d, t, xbuf, xn, hT, pb, xres_name, xn_name, hT_name):
        sch.add("sp", lambda e: e.dma_start(out=xbuf, in_=src_d[128 * t:128 * (t + 1), :]),
                writes=[xres_name], chan=xres_name)
        ss, ssn = stat_slot()
        r1, r1n = stat_slot()
        sch.add("act", lambda e: e.activation(out=xn, in_=xbuf, func=AF.Square, accum_out=ss),
                reads=[xres_name], writes=[xn_name, ssn])
        sch.add("act", lambda e: e.activation(out=r1, in_=ss, func=AF.Ln, scale=1.0 / D, bias=eps_col),
                reads=[ssn, "eps_col"], writes=[r1n])
        sch.add("act", lambda e: e.activation(out=r1, in_=r1, func=AF.Exp, scale=-0.5),
                reads=[r1n], writes=[r1n])
        sch.add("dve", lambda e: e.tensor_scalar(out=xn, in0=xbuf, scalar1=r1, scalar2=None, op0=ALU.mult),
                reads=[xres_name, r1n], writes=[xn_name])

        def tr(e):
            pbf = bank_bf(pb)
            for kc in range(8):
                ins = e.transpose(pbf[:, 128 * kc:128 * (kc + 1)], xn[:, 128 * kc:128 * (kc + 1)], ident_bf)
            return ins
        sch.add("pe", tr, reads=[xn_name, "ident_bf"], writes=[("ps", pb)])
        for kc in range(8):
            sch.add("dve", lambda e, kc=kc: e.tensor_scalar(
                out=hT[:, kc, :], in0=bank_bf(pb)[:, 128 * kc:128 * (kc + 1)],
                scalar1=gs_pp[:, kc:kc + 1], scalar2=sh_pp[:, kc:kc + 1], op0=ALU.mult, op1=ALU.add),
                reads=[("ps", pb), "gs_pp", "sh_pp"], writes=[hT_name])

    eps_col = ar.take(F32, [128, 1]) if False else None

    ar.reset(PERSIST_END)
    eps_col = ar.take(F32, [128, 1])
    one_col = ar.take(F32, [128, 1])
    nshift_col = ar.take(F32, [128, 1])
    gate_p = ar.take(F32, [128, 1024])
    PERSIST_END = ar.off
    sch.add("dve", lambda e: e.memset(eps_col, EPS), writes=["eps_col"])
    sch.add("dve", lambda e: e.memset(one_col, 1.0), writes=["one_col"])
    sch.add("dve", lambda e: e.memset(nshift_col, -EXP_SHIFT), writes=["nshift_col"])

    def gmlp_layer(l):
        src_d = x_d if l == 0 else out_d
        barrier()
        ar.reset(PERSIST_END)
        Win = ar.take(BF16, [128, 8, 6144])
        Wout = ar.take(BF16, [128, 16, 1024])
        WsT = ar.take(BF16, [128, 8, 128])
        bsb = ar.take(F32, [128, 8, 128])
        sgu = ar.take(F32, [128, 2048])
        WORK = ar.off
        ada_bc = ada_phase(a_ada_w[l], a_ada_b[l], 3072, WORK)
        ada_pp(a_norm_g[l], ada_bc)
        ada_last = [sch.chan_last[("aw", b_, h_)] for b_ in range(2) for h_ in range(2)]
        wv = a_w_in[l].rearrange("(c p) n -> p c n", p=128)
        for h in (1, 0, 2):
            for kc in range(8):
                sch.add("pool", lambda e, kc=kc, h=h: e.dma_start(
                    out=Win[:, kc, 2048 * h:2048 * (h + 1)], in_=wv[:, kc, 2048 * h:2048 * (h + 1)]),
                    writes=[("Win", h, kc)], chan="win%d" % h, extra_deps=ada_last)
        wo = a_w_out[l].rearrange("(c p) n -> p c n", p=128)
        for ec in range(0, 16, 2):
            sch.add("pool", lambda e, ec=ec: e.dma_start(out=Wout[:, ec:ec + 2, :], in_=wo[:, ec:ec + 2, :]),
                    writes=[("Wout", ec), ("Wout", ec + 1)], chan="wout", extra_deps=ada_last)
        sch.add("sp", lambda e: e.dma_start(out=bsb, in_=a_bsb[l]), writes=["bsb"], chan=uchan())
        sch.add("sp", lambda e: e.dma_start(out=sgu, in_=a_sgu[l]), writes=["sgu"], chan=uchan())
        wst_f = ar.take(F32, [128, 8, 128])
        sch.add("sp", lambda e: e.dma_start(out=wst_f, in_=a_wsT[l]), writes=["wst_f"], chan=uchan())
        sch.add("dve", lambda e: e.tensor_tensor(
            out=WsT, in0=wst_f, in1=triu_f.unsqueeze(1).to_broadcast([128, 8, 128]), op=ALU.mult),
            reads=["wst_f", "triu_f"], writes=["WsT"])
        sch.add("dve", lambda e: e.tensor_scalar(out=gate_p, in0=ada_bc[:, 2048:3072], scalar1=0.5,
                                                  scalar2=None, op0=ALU.mult), reads=["ada_bc"], writes=["gate_p"])
        barrier(keep=("Win", "Wout"), skip_chans=("win0", "win1", "win2", "wout"))
        for ec in range(16):
            sch.add("pool" if ec % 2 else "dve", lambda e, ec=ec: e.tensor_tensor(
                out=Wout[:, ec, :], in0=Wout[:, ec, :], in1=gate_p, op=ALU.mult),
                reads=[("Wout", k_) for k_ in range(16)], writes=[("WoutF", ec)])
        WoutF = [("WoutF", ec) for ec in range(16)]
        WinV = [("Win", 1, kc) for kc in range(8)]
        WinU = [("Win", 0, kc) for kc in range(8)]
        WinZ = [("Win", 2, kc) for kc in range(8)]
        ar.reset(WORK)
        NXB = 3
        xb = [ar.take(F32, [128, 1024]) for _ in range(NXB)]
        xn = ar.take(BF16, [128, 1024])
        hT = [ar.take(BF16, [128, 8, 128]) for _ in range(2)]
        gv = ar.take(F32, [128, 2048])
        vn = [ar.take(BF16, [128, 2048]) for _ in range(2)]
        gu = ar.take(BF16, [128, 2048])
        szb = [ar.take(F32, [128, 512]) for _ in range(2)]
        gg = ar.take(BF16, [128, 2048])
        t3 = [ar.take(F32, [128, 512]) for _ in range(2)]
        yT = [ar.take(BF16, [128, 16, 128]) for _ in range(2)]
        rot = [0]

        def nextA():
            b = rot[0] % 4
            rot[0] += 1
            return b

        def do_front(t):
            b = t % NXB
            front(src_d, t, xb[b], xn, hT[t % 2], nextA(), ("xb", b), "xn", ("hT", t % 2))

        do_front(0)
        for t in range(NT):
            b = t % NXB
            hTt = hT[t % 2]
            hTn = ("hT", t % 2)
            vnt = vn[t % 2]
            vnn = ("vn", t % 2)
            ssv, ssvn = stat_slot()
            for j in range(4):
                pb = nextA()

                def mm(e, j=j, pb=pb, hTt=hTt):
                    for kc in range(8):
                        ins = e.matmul(bank(pb), lhsT=hTt[:, kc, :], rhs=Win[:, kc, 2048 + 512 * j:2048 + 512 * (j + 1)],
                                       start=(kc == 0), stop=(kc == 7))
                    return ins
                sch.add("pe", mm, reads=[hTn] + WinV, writes=[("ps", pb)])
                sch.add("act", lambda e, j=j, pb=pb: e.activation(
                    out=gv[:, 512 * j:512 * (j + 1)], in_=bank(pb), func=AF.Gelu_apprx_tanh),
                    reads=[("ps", pb)], writes=[("gv", j)])
            sch.add("act", lambda e, ssv=ssv, vnt=vnt: e.activation(out=vnt, in_=gv, func=AF.Square, accum_out=ssv),
                    reads=[("gv", j) for j in range(4)], writes=[vnn, ssvn])
            rv, rvn = stat_slot()
            sch.add("act", lambda e, rv=rv, ssv=ssv: e.activation(out=rv, in_=ssv, func=AF.Ln, scale=1.0 / 2048, bias=eps_col),
                    reads=[ssvn, "eps_col"], writes=[rvn])
            sch.add("act", lambda e, rv=rv: e.activation(out=rv, in_=rv, func=AF.Exp, scale=-0.5),
                    reads=[rvn], writes=[rvn])
            sch.add("dve", lambda e, rv=rv, vnt=vnt: e.scalar_tensor_tensor(
                out=vnt, in0=gv, scalar=rv, in1=sgu, op0=ALU.mult, op1=ALU.mult),
                reads=[("gv", j) for j in range(4)] + [rvn, "sgu"], writes=[vnn])
            if t + 1 < NT:
                do_front(t + 1)
            for j in range(4):
                pb = nextA()

                def mm(e, j=j, pb=pb, hTt=hTt):
                    for kc in range(8):
                        ins = e.matmul(bank(pb), lhsT=hTt[:, kc, :], rhs=Win[:, kc, 512 * j:512 * (j + 1)],
                                       start=(kc == 0), stop=(kc == 7))
                    return ins
                sch.add("pe", mm, reads=[hTn] + WinU, writes=[("ps", pb)])
                sch.add("act", lambda e, j=j, pb=pb: e.activation(
                    out=gu[:, 512 * j:512 * (j + 1)], in_=bank(pb), func=AF.Gelu_apprx_tanh),
                    reads=[("ps", pb)], writes=[("gu", j)])
            mix_banks = []
            for j in range(4):
                pb = nextA()

                def mm(e, j=j, pb=pb, hTt=hTt):
                    for kc in range(8):
                        ins = e.matmul(bank(pb), lhsT=hTt[:, kc, :], rhs=Win[:, kc, 4096 + 512 * j:4096 + 512 * (j + 1)],
                                       start=(kc == 0), stop=(kc == 7))
                    return ins
                sch.add("pe", mm, reads=[hTn] + WinZ, writes=[("ps", pb)])
                sb = j % 2
                sch.add("act", lambda e, sb=sb, pb=pb: e.activation(out=szb[sb], in_=bank(pb), func=AF.Tanh, scale=0.5),
                        reads=[("ps", pb)], writes=[("szb", sb)])
                sch.add("dve", lambda e, sb=sb, pb=pb: e.scalar_tensor_tensor(
                    out=szb[sb], in0=szb[sb], scalar=1.0, in1=bank(pb), op0=ALU.add, op1=ALU.mult),
                    reads=[("ps", pb), ("szb", sb)], writes=[("szb", sb)])
                sch.add("pool", lambda e, j=j, sb=sb: e.tensor_tensor(
                    out=gg[:, 512 * j:512 * (j + 1)], in0=gu[:, 512 * j:512 * (j + 1)], in1=szb[sb], op=ALU.mult),
                    reads=[("gu", j), ("szb", sb)], writes=[("gg", j)])
            yTt = yT[t % 2]
            yTn = ("yT", t % 2)
            for q in range(4):
                pm = 4 + (q % 2)
                pg = 6 + (q % 2)

                def mix(e, q=q, pm=pm, vnt=vnt):
                    for i in range(4):
                        ec = 4 * q + i
                        ins = e.matmul(bank(pm)[:, 128 * i:128 * (i + 1)], lhsT=vnt[:, 128 * ec:128 * (ec + 1)],
                                       rhs=WsT[:, ec // 2, :], start=True, stop=True)
                    return ins
                sch.add("pe", mix, reads=[vnn, "WsT"], writes=[("ps", pm)])

                def trg(e, q=q, pg=pg):
                    pbf = bank_bf(pg)
                    for i in range(4):
                        ec = 4 * q + i
                        ins = e.transpose(pbf[:, 128 * i:128 * (i + 1)], gg[:, 128 * ec:128 * (ec + 1)], ident_bf)
                    return ins
                sch.add("pe", trg, reads=[("gg", q), "ident_bf"], writes=[("ps", pg)])
                tb = q % 2
                bs_v = bsb[:, 2 * q:2 * q + 2, :].unsqueeze(2).to_broadcast([128, 2, 2, 128])
                sch.add("dve", lambda e, pm=pm, tb=tb, bs_v=bs_v: e.tensor_tensor(
                    out=t3[tb].rearrange("p (a b c) -> p a b c", a=2, b=2), in0=bank(pm).rearrange("p (a b c) -> p a b c", a=2, b=2),
                    in1=bs_v, op=ALU.add),
                    reads=[("ps", pm), "bsb"], writes=[("t3", tb)])
                sch.add("dve", lambda e, q=q, pg=pg, tb=tb, yTt=yTt: e.tensor_tensor(
                    out=yTt[:, 4 * q:4 * q + 4, :], in0=bank_bf(pg)[:, 0:512].rearrange("p (a b) -> p a b", b=128),
                    in1=t3[tb].rearrange("p (a b) -> p a b", b=128), op=ALU.mult),
                    reads=[("ps", pg), ("t3", tb)], writes=[yTn])
            for h in range(2):
                pb = nextA()

                def mo(e, h=h, pb=pb, yTt=yTt):
                    for ec in range(16):
                        ins = e.matmul(bank(pb), lhsT=yTt[:, ec, :], rhs=Wout[:, ec, 512 * h:512 * (h + 1)],
                                       start=(ec == 0), stop=(ec == 15))
                    return ins
                sch.add("pe", mo, reads=[yTn] + WoutF, writes=[("ps", pb)])
                sch.add("dve", lambda e, h=h, pb=pb, b=b: e.tensor_tensor(
                    out=xb[b][:, 512 * h:512 * (h + 1)], in0=bank(pb), in1=xb[b][:, 512 * h:512 * (h + 1)], op=ALU.add),
                    reads=[("ps", pb), ("xb", b)], writes=[("xb", b)])
            sch.add("sp", lambda e, t=t, b=b: e.dma_start(out=out_d[128 * t:128 * (t + 1), :], in_=xb[b]),
                    reads=[("xb", b)], writes=[("outrow", t)], chan=("st", b))

    for l in range(n_a):
        gmlp_layer(l)

    DIL = (1, 4, 16)
    STW = 256
    NST = S // STW
    kvs = {}

    STA = 512
    NSTA = S // STA

    def dil_view(ap2d, d):
        return ap2d.rearrange("p (q r) -> p r q", r=d)

    def norm_to(pb, pb2, sq, rs, gcol, pieces, nm, qraw=None, dst_res=None):
        if qraw is None:
            sch.add("act", lambda e: e.activation(out=sq, in_=bank(pb), func=AF.Square),
                    reads=[("ps", pb)], writes=[nm + "sq"])
            src = bank(pb)
            src_res = ("ps", pb)
        else:
            sch.add("dve", lambda e: e.tensor_copy(out=qraw, in_=bank(pb)), reads=[("ps", pb)], writes=[nm + "raw"])
            sch.add("pool", lambda e: e.tensor_tensor(out=sq, in0=qraw, in1=qraw, op=ALU.mult),
                    reads=[nm + "raw"], writes=[nm + "sq"])
            src = qraw
            src_res = nm + "raw"
        sch.add("pe", lambda e: e.matmul(bank(pb2), lhsT=ones_bf, rhs=sq, start=True, stop=True),
                reads=[nm + "sq", "ones_bf"], writes=[("ps", pb2)])
        sch.add("act", lambda e: e.activation(out=rs, in_=bank(pb2), func=AF.Ln, scale=1.0 / 128, bias=eps_col),
                reads=[("ps", pb2), "eps_col"], writes=[nm + "rs"])
        sch.add("act", lambda e: e.activation(out=rs, in_=rs, func=AF.Exp, scale=-0.5),
                reads=[nm + "rs"], writes=[nm + "rs"])
        for (dst_view, c0, w, d) in pieces:
            sch.add("dve", lambda e, dst_view=dst_view, c0=c0, w=w, d=d: e.scalar_tensor_tensor(
                out=dst_view, in0=dil_view(src[:, c0:c0 + w], d), scalar=gcol, in1=dil_view(rs[:, c0:c0 + w], d),
                op0=ALU.mult, op1=ALU.mult), reads=[src_res, nm + "rs", "gcol"],
                writes=[nm + "dst"] + ([dst_res] if dst_res is not None else []))

    def kv_phase():
        barrier()
        ar.reset(PERSIST_END - 4096)
        KT = ar.take(BF16, [128, 6, 4096])
        Vst = ar.take(BF16, [128, 6, 32, 128])
        kvs.update(KT=KT, Vst=Vst, END=ar.off)
        Wkv = ar.take(BF16, [128, 8, 1536])
        VT = ar.take(BF16, [128, 6, 4096])
        VT_OFF = ar.off - 6 * 4096 * 2
        WORK = ar.off
        wv = w_kv.rearrange("(c p) n -> p c n", p=128)
        for kc in range(8):
            sch.add("pool", lambda e, kc=kc: e.dma_start(out=Wkv[:, kc, :], in_=wv[:, kc, :]),
                    writes=[("Wkv", kc)], chan="win")
        sch.add("sp", lambda e: e.dma_start(out=kg_pp, in_=k_norm_g[:, :]), writes=["kg_pp"], chan=uchan())
        ada_bc = ada_phase(kv_ada_w, kv_ada_b, 2048, VT_OFF)
        ada_pp(kv_norm_g[:, :], ada_bc)
        barrier(keep=("Wkv",), skip_chans=("win",))
        ar.reset(WORK)
        xb = [ar.take(F32, [128, 1024]) for _ in range(2)]
        xn = ar.take(BF16, [128, 1024])
        sq = [ar.take(BF16, [128, STA]) for _ in range(2)]
        rs = [ar.take(F32, [128, STA]) for _ in range(2)]
        NHK = 2 if ar.off + 2 * 8192 <= ARENA_BYTES else 1
        hTs = [ar.take(BF16, [128, 8, STA]) for _ in range(NHK)]
        if _os.environ.get("MK_DEBUG"):
            print("KV arena used", ar.off, "NHK", NHK, flush=True)
        cnt = [0]
        fb = [0]
        for m in range(NSTA):
            hT = hTs[m % NHK]
            hb = m % NHK
            for tt in range(4):
                t = 4 * m + tt
                b = t % 2
                front(out_d, t, xb[b], xn, hT[:, :, 128 * tt:128 * (tt + 1)], 6 + (fb[0] % 2), ("xb", b), "xn", ("hT", hb, tt))
                fb[0] += 1
            for j in range(12):
                k = cnt[0] % 2
                pb = cnt[0] % 4
                cnt[0] += 1
                pb2 = 4 + k
                g = (j % 6) // 2
                d = DIL[g]

                def mm(e, j=j, pb=pb, hT=hT):
                    for kc in range(8):
                        ins = e.matmul(bank(pb), lhsT=Wkv[:, kc, 128 * j:128 * (j + 1)], rhs=hT[:, kc, :],
                                       start=(kc == 0), stop=(kc == 7))
                    return ins
                sch.add("pe", mm, reads=[("hT", hb, i_) for i_ in range(4)] + [("Wkv", kc_) for kc_ in range(8)],
                        writes=[("ps", pb)])
                q0 = STA * m // d
                if j < 6:
                    dst = KT[:, j, :].rearrange("p (r q) -> p r q", r=d)[:, :, q0:q0 + STA // d]
                    norm_to(pb, pb2, sq[k], rs[k], kg_pp, [(dst, 0, STA, d)], "k%d" % k)
                else:
                    dst = VT[:, j - 6, :].rearrange("p (r q) -> p r q", r=d)[:, :, q0:q0 + STA // d]
                    sch.add("dve", lambda e, pb=pb, dst=dst, d=d: e.tensor_copy(
                        out=dst, in_=dil_view(bank(pb), d)),
                        reads=[("ps", pb)], writes=["VT"])
        for j in range(6):
            for q in range(4):
                pb = q % 4

                def tr(e, j=j, q=q, pb=pb):
                    for i in range(8):
                        blk = 8 * q + i
                        ins = e.transpose(bank_bf(pb)[:, 128 * i:128 * (i + 1)], VT[:, j, 128 * blk:128 * (blk + 1)], ident_bf)
                    return ins
                sch.add("pe", tr, reads=["VT"], writes=[("ps", pb)])
                sch.add("dve" if q % 2 else "act", (lambda e, j=j, q=q, pb=pb: e.tensor_copy(
                    out=Vst[:, j, 8 * q:8 * q + 8, :], in_=bank_bf(pb).rearrange("p (a b) -> p a b", b=128))) if q % 2 else
                    (lambda e, j=j, q=q, pb=pb: e.activation(
                        out=Vst[:, j, 8 * q:8 * q + 8, :], in_=bank_bf(pb).rearrange("p (a b) -> p a b", b=128), func=AF.Copy)),
                    reads=[("ps", pb)], writes=[("Vst", j, q)])

    def attn_layer(l):
        KT, Vst = kvs["KT"], kvs["Vst"]
        barrier()
        ar.reset(kvs["END"])
        WQ_OFF = ar.off
        Wq = ar.take(BF16, [128, 8, 4096])
        WORK = ar.off
        ada_bc = ada_phase(b_ada_w[l], b_ada_b[l], 3072, WORK, cw=256)
        ada_pp(b_norm_g[l], ada_bc)
        sch.add("sp", lambda e: e.dma_start(out=gate_d[:, :], in_=ada_bc[:, 2048:3072]), reads=["ada_bc"],
                writes=["gate_d"], chan=uchan())
        sch.add("sp", lambda e: e.dma_start(out=qg_pp, in_=b_qg[l]), writes=["qg_pp"], chan=uchan())
        sch.add("dve", lambda e: e.tensor_scalar(out=qg_pp, in0=qg_pp, scalar1=float(128 ** -0.5), scalar2=None, op0=ALU.mult),
                reads=["qg_pp"], writes=["qg_pp"])
        ada_last = [sch.chan_last[("aw", b_, h_)] for b_ in range(2) for h_ in range(2)]
        wv = b_w_in[l].rearrange("(c p) n -> p c n", p=128)
        for cg in range(4):
            for kc in range(8):
                sch.add("pool", lambda e, kc=kc, cg=cg: e.dma_start(
                    out=Wq[:, kc, 1024 * cg:1024 * (cg + 1)], in_=wv[:, kc, 1024 * cg:1024 * (cg + 1)]),
                    writes=[("Wq", cg, kc)], chan="wq%d" % cg, extra_deps=ada_last)
        barrier(keep=("Wq",), skip_chans=("wq0", "wq1", "wq2", "wq3"))
        ar.reset(WORK)
        xb = [ar.take(F32, [128, 1024]) for _ in range(2)]
        xn = ar.take(BF16, [128, 1024])
        NHT = int(_os.environ.get("MK_NHT", "2"))
        hTs = [ar.take(BF16, [128, 8, STA]) for _ in range(NHT)]
        NKB = int(_os.environ.get("MK_NKB", "2"))
        sq = [ar.take(BF16, [128, STA]) for _ in range(NKB)]
        rs = [ar.take(F32, [128, STA]) for _ in range(NKB)]
        NQS = 4
        qst = [ar.take(BF16, [128, 2, STW]) for _ in range(NQS)]
        szst = ar.take(BF16, [128, 8, STA])
        qraw = [ar.take(F32, [128, STA]) for _ in range(NKB)] if QRAW else [None] * NKB
        if _os.environ.get("MK_DEBUG"):
            print("pass A arena used", ar.off, "of", ARENA_BYTES, flush=True)
        cnt = [0]
        qc = [0]
        fb = [0]
        for m in range(NSTA):
            hT = hTs[m % NHT]
            hb = m % NHT
            for tt in range(4):
                t = 4 * m + tt
                b = t % 2
                front(out_d, t, xb[b], xn, hT[:, :, 128 * tt:128 * (tt + 1)], 6 + (fb[0] % 2), ("xb", b), "xn", ("hT", hb, tt))
                fb[0] += 1
            hTr = [("hT", hb, i_) for i_ in range(4)]
            for h in range(24):
                k = cnt[0] % NKB
                pb = cnt[0] % 4
                pb2 = 4 + (cnt[0] % 2)
                cnt[0] += 1
                d = DIL[h // 8]

                def mm(e, h=h, pb=pb, hT=hT):
                    for kc in range(8):
                        ins = e.matmul(bank(pb), lhsT=Wq[:, kc, 128 * h:128 * (h + 1)], rhs=hT[:, kc, :],
                                       start=(kc == 0), stop=(kc == 7))
                    return ins
                sch.add("pe", mm, reads=hTr + [("Wq", h // 8, kc_) for kc_ in range(8)], writes=[("ps", pb)])
                qi = qc[0] % NQS
                qc[0] += 1
                pieces = [(qst[qi][:, hf, :].rearrange("p (r q) -> p r q", r=d), STW * hf, STW, d) for hf in range(2)]
                norm_to(pb, pb2, sq[k], rs[k], qg_pp, pieces, "k%d" % k, qraw=qraw[k], dst_res=("qst", qi))
                sch.add("sp", lambda e, h=h, m=m, qi=qi: e.dma_start(
                    out=qs_d[h, 2 * m:2 * m + 2].rearrange("a p q -> p a q"), in_=qst[qi]),
                    reads=[("qst", qi)], writes=[("qs_d", h, m)], chan=("qst", qi))
            for a in range(8):
                pb = 6 + (fb[0] % 2)
                fb[0] += 1

                def mz(e, a=a, pb=pb, hT=hT):
                    for kc in range(8):
                        ins = e.matmul(bank(pb), lhsT=Wq[:, kc, 3072 + 128 * a:3072 + 128 * (a + 1)], rhs=hT[:, kc, :],
                                       start=(kc == 0), stop=(kc == 7))
                    return ins
                sch.add("pe", mz, reads=hTr + [("Wq", 3, kc_) for kc_ in range(8)], writes=[("ps", pb)])
                sch.add("act", lambda e, a=a, pb=pb: e.activation(out=szst[:, a, :], in_=bank(pb), func=AF.Silu),
                        reads=[("ps", pb)], writes=[("szst", a)])
            for hf in range(2):
                sch.add("sp", lambda e, m=m, hf=hf: e.dma_start(out=szs_d[2 * m + hf], in_=szst[:, :, STW * hf:STW * (hf + 1)]),
                        reads=[("szst", a) for a in range(8)], writes=[("szst", a) for a in range(8)], chan=("szst", hf))

        barrier()
        ar.reset(kvs["END"])
        Wo = ar.take(BF16, [128, 8, 1024])
        etab = ar.take(BF16, [128, 12, 512])
        for h in range(3):
            sch.add("pool", lambda e, h=h: e.dma_start(out=etab[:, 4 * h:4 * h + 4, :], in_=etab_d[:, 4 * h:4 * h + 4, :]),
                    writes=[("etab", h)], chan="petab")
        wo = b_w_out[l].rearrange("(c p) n -> p c n", p=128)
        for a in range(0, 8, 2):
            sch.add("pool", lambda e, a=a: e.dma_start(out=Wo[:, a:a + 2, :], in_=wo[:, a:a + 2, :]),
                    writes=[("Wo", a)], chan="win")
        gate_bc = ar.take(F32, [128, 1024])
        sch.add("sp", lambda e: e.dma_start(out=gate_bc, in_=gate_d[:, :]), writes=["gate_bc"], chan=uchan())
        for a in range(8):
            sch.add("dve", lambda e, a=a: e.tensor_tensor(out=Wo[:, a, :], in0=Wo[:, a, :], in1=gate_bc, op=ALU.mult),
                    reads=[("Wo", 0), ("Wo", 2), ("Wo", 4), ("Wo", 6), "gate_bc"], writes=[("WoF", a)])
        barrier()
        NQT = 3
        NPT = int(_os.environ.get("MK_NPT", "8"))
        QT = [ar.take(BF16, [128, 4, STW]) for _ in range(NQT)]
        PT = [ar.take(BF16, [128, 512]) for _ in range(NPT)]
        accO2 = [ar.take(F32, [128, 4, STW]) for _ in range(2)]
        accL2 = [ar.take(F32, [128, 4, STW]) for _ in range(2)]
        rl = ar.take(F32, [128, 4, STW])
        szb = [ar.take(BF16, [128, 8, STW]) for _ in range(2)]
        yT = [ar.take(BF16, [128, 8, STW]) for _ in range(2)]
        xb = [ar.take(F32, [128, 1024]) for _ in range(2)]
        qcnt = [0]
        pcnt = [0]
        scnt = [0]
        ocnt = [0]
        pending = []
        DEFER_G = int(_os.environ.get("MK_DEFER_G", "2"))
        EM_MOD = int(_os.environ.get("MK_EM_MOD", "3"))
        EM_POOL = int(_os.environ.get("MK_EM_POOL", "2"))
        for m in range(NST):
            zb = m % 2
            sch.add("sp", lambda e, m=m, zb=zb: e.dma_start(out=szb[zb], in_=szs_d[m]),
                    writes=[("szb", zb)], chan=("szb", zb))
            yTm = yT[m % 2]
            yTn = ("yT", m % 2)
            for kh in range(2):
                accO = accO2[kh]
                accL = accL2[kh]
                nO = ("accO", kh)
                nL = ("accL", kh)
                for g in range(3):
                    if kh == 0 and g == DEFER_G and pending:
                        pending.pop(0)()
                    d = DIL[g]
                    j = 2 * g + kh
                    qb = qcnt[0] % NQT
                    qcnt[0] += 1
                    h0 = 8 * g + 4 * kh
                    sch.add("sp", lambda e, qb=qb, h0=h0, m=m: e.dma_start(
                        out=QT[qb], in_=qs_d[h0:h0 + 4, m].rearrange("a p q -> p a q")),
                        writes=[("QT", qb)], chan=("QT", qb))
                    nqu = min(128, STW // d)
                    U = 512 // (4 * nqu)
                    for pk in range(2):
                        if d == 1:
                            units = [(0, 2 * m + pk, 128 * pk)]
                            i0 = 0
                        else:
                            n = (STW * m // d) // 128
                            i0 = (STW * m // d) % 128
                            units = [(r, n, r * nqu) for r in range(pk * U, (pk + 1) * U)]
                        n_blk = units[0][1]
                        chunks = [1] if n_blk == 0 else [0, 1]
                        pts = {}
                        for c in chunks:
                            sb = scnt[0] % 4
                            scnt[0] += 1
                            pt_i = pcnt[0] % NPT
                            pcnt[0] += 1
                            pts[c] = pt_i

                            def qk(e, units=units, c=c, sb=sb, j=j, qb=qb, nqu=nqu, d=d):
                                for ui, (r, n, col) in enumerate(units):
                                    kb = r * (32 // d) + n - 1 + c
                                    ins = e.matmul(
                                        bank(sb)[:, 4 * nqu * ui:4 * nqu * (ui + 1)].rearrange("p (a q) -> p a q", a=4),
                                        lhsT=KT[:, j, 128 * kb:128 * (kb + 1)], rhs=QT[qb][:, :, col:col + nqu],
                                        start=True, stop=True)
                                return ins
                            sch.add("pe", qk, reads=[("QT", qb)], writes=[("ps", sb)])
                            sch.add("act", lambda e, sb=sb, pt_i=pt_i: e.activation(
                                out=PT[pt_i], in_=bank(sb), func=AF.Exp, bias=nshift_col, scale=1.0),
                                reads=[("ps", sb), "nshift_col"], writes=[("PT", pt_i)])
                            ev = etab[:, (g * 2 + kh) * 2 + c, :].rearrange("p (a i) -> p a i", a=4)[:, :, i0:i0 + nqu]
                            ev = ev.unsqueeze(1).to_broadcast([128, U, 4, nqu])
                            ptv = PT[pt_i].rearrange("p (u a q) -> p u a q", u=U, a=4)
                            sch.add("pool" if (pcnt[0] % EM_MOD) < EM_POOL else "dve", lambda e, ptv=ptv, ev=ev: e.tensor_tensor(
                                out=ptv, in0=ptv, in1=ev, op=ALU.mult),
                                reads=[("PT", pt_i)], writes=[("PT", pt_i)])
                        ob = 4 + (ocnt[0] % 2)
                        lb = 6 + (ocnt[0] % 2)
                        ocnt[0] += 1

                        def pv(e, units=units, chunks=chunks, pts=pts, ob=ob, j=j, nqu=nqu, d=d, ones=False):
                            for ui, (r, n, col) in enumerate(units):
                                for ci, c in enumerate(chunks):
                                    kb = r * (32 // d) + n - 1 + c
                                    lhs = ones_bf if ones else Vst[:, j, kb, :]
                                    ins = e.matmul(bank(ob)[:, 4 * nqu * ui:4 * nqu * (ui + 1)], lhsT=lhs,
                                                   rhs=PT[pts[c]][:, 4 * nqu * ui:4 * nqu * (ui + 1)],
                                                   start=(ci == 0), stop=(ci == len(chunks) - 1))
                            return ins
                        rds = [("PT", pts[c]) for c in chunks]
                        sch.add("pe", pv, reads=rds, writes=[("ps", ob)])
                        sch.add("pe", lambda e, pv=pv, lb=lb: pv(e, ob=lb, ones=True), reads=rds, writes=[("ps", lb)])
                        for (acc, bnk, nm) in ((accO, ob, nO), (accL, lb, nL)):
                            if d == 1:
                                av = acc[:, :, 128 * pk:128 * (pk + 1)].unsqueeze(1)
                            else:
                                av = acc.rearrange("p a (q r) -> p r a q", r=d)[:, pk * U:(pk + 1) * U, :, :]
                            bv = bank(bnk).rearrange("p (u a q) -> p u a q", u=U, a=4)
                            if g == 0:
                                sch.add("act", lambda e, av=av, bv=bv: e.activation(out=av, in_=bv, func=AF.Copy),
                                        reads=[("ps", bnk)], writes=[nm])
                            else:
                                sch.add("dve", lambda e, av=av, bv=bv: e.tensor_tensor(out=av, in0=bv, in1=av, op=ALU.add),
                                        reads=[("ps", bnk), nm], writes=[nm])
                sch.add("act", lambda e, accL=accL: e.activation(out=rl, in_=accL, func=AF.Ln), reads=[nL], writes=["rl"])
                sch.add("act", lambda e: e.activation(out=rl, in_=rl, func=AF.Exp, scale=-1.0), reads=["rl"], writes=["rl"])
                sch.add("dve", lambda e, accO=accO: e.tensor_tensor(out=accO, in0=accO, in1=rl, op=ALU.mult),
                        reads=[nO, "rl"], writes=[nO])
                sch.add("pool", lambda e, kh=kh, zb=zb, yTm=yTm, accO=accO: e.tensor_tensor(
                    out=yTm[:, 4 * kh:4 * kh + 4, :], in0=accO, in1=szb[zb][:, 4 * kh:4 * kh + 4, :], op=ALU.mult),
                    reads=[nO, ("szb", zb)], writes=[(yTn, kh)])
            def o_stage(m=m, yTm=yTm, yTn=yTn):
                for tt in range(2):
                    t = 2 * m + tt
                    b = t % 2
                    sch.add("sp", lambda e, t=t, b=b: e.dma_start(out=xb[b], in_=out_d[128 * t:128 * (t + 1), :]),
                            writes=[("xb", b)], chan=("xb", b))
                    for h in range(2):
                        sb = scnt[0] % 4
                        scnt[0] += 1

                        def mo(e, h=h, sb=sb, tt=tt, yTm=yTm):
                            for a in range(8):
                                ins = e.matmul(bank(sb), lhsT=yTm[:, a, 128 * tt:128 * (tt + 1)], rhs=Wo[:, a, 512 * h:512 * (h + 1)],
                                               start=(a == 0), stop=(a == 7))
                            return ins
                        sch.add("pe", mo, reads=[(yTn, 0), (yTn, 1)], writes=[("ps", sb)])
                        sch.add("dve", lambda e, h=h, sb=sb, b=b: e.tensor_tensor(
                            out=xb[b][:, 512 * h:512 * (h + 1)], in0=bank(sb), in1=xb[b][:, 512 * h:512 * (h + 1)], op=ALU.add),
                            reads=[("ps", sb), ("xb", b)], writes=[("xb", b)])
                    sch.add("sp", lambda e, t=t, b=b: e.dma_start(out=out_d[128 * t:128 * (t + 1), :], in_=xb[b]),
                            reads=[("xb", b)], writes=[("outrow", t)], chan=("st", b))
            pending.append(o_stage)
        while pending:
            pending.pop(0)()

    if do_kv:
        kv_phase()
    for l in range(n_b):
        attn_layer(l)

    sch.finish()

    sch.resolve()
    if _os.environ.get("MK_DEBUG"):
        print("sim_end_us", sch.sim_end, "n_ops", len(sch.ops), flush=True)
        sch.report()
        if globals().get("_DUMP"):
            lo, hi = globals()["_DUMP"]
            for i in sch.order:
                op = sch.ops[i]
                st = sch.sim_start[i]
                if lo <= st <= hi and op["fn"] is not None:
                    print("%9.2f %6.2f %-5s %s" % (st, sch.sim_endt[i] - st, op["issuer"], op["tag"]))
    sems = {}
    for k in sch.keys:
        nm = "s_" + (k if isinstance(k, str) else "_".join(str(z) for z in (k[1] if isinstance(k[1], tuple) else (k[1],))))
        sems[k] = es.enter_context(nc.semaphore(nm))
    block = es.enter_context(nc.Block())

    @block.tensor
    def _(e):
        sch.emit("pe", e, sems)

    @block.scalar
    def _(e):
        sch.emit("act", e, sems)

    @block.vector
    def _(e):
        sch.emit("dve", e, sems)

    @block.gpsimd
    def _(e):
        sch.emit("pool", e, sems)

    @block.sync
    def _(e):
        sch.emit("sp", e, sems)

    es.close()
    return nc


def _pp(v, n):
    return np.ascontiguousarray(np.asarray(v, np.float32).reshape(n, 128).T)


def _bc(v):
    v = np.asarray(v, np.float32).reshape(1, -1)
    return np.ascontiguousarray(np.broadcast_to(v, (128, v.shape[1])))


def _etab():
    H = 24
    slopes = np.exp2(-8.0 * np.arange(1, H + 1, dtype=np.float64) / H)
    dil = (1, 4, 16)
    j = np.arange(128)[:, None]
    i = np.arange(128)[None, :]
    E = np.zeros((128, 12, 512), np.float64)
    for g in range(3):
        for kh in range(2):
            for c in range(2):
                dq = (128 + i - j) if c == 0 else (i - j)
                valid = (dq >= 0) & (dq <= 128)
                for a in range(4):
                    h = 8 * g + 4 * kh + a
                    v = np.where(valid, np.exp(-slopes[h] * dil[g] * dq), 0.0)
                    E[:, (g * 2 + kh) * 2 + c, a * 128:(a + 1) * 128] = v
    return E.astype(np.float32)


def make_in_maps(inp):
    f = lambda k: np.asarray(inp[k], np.float32)
    shared = {
        "ident": np.eye(128, dtype=np.float32),
        "triu": np.triu(np.ones((128, 128), np.float32)),
        "etab": _etab(),
        "a_ada_w": f("a_ada_w"),
        "a_ada_b_bc": np.stack([_bc(f("a_ada_b")[l]) for l in range(2)]),
        "a_norm_g_pp": np.stack([_pp(f("a_norm_g")[l], 8) for l in range(2)]),
        "a_w_in": f("a_w_in"),
        "a_sgu_bc": np.stack([_bc(f("a_sgu_g")[l]) for l in range(2)]),
        "a_wsT": np.ascontiguousarray(np.transpose(f("a_w_spatial"), (0, 3, 1, 2))),
        "a_bsb": np.stack([np.ascontiguousarray(np.broadcast_to(f("a_b_spatial")[l][None], (128, 8, 128)))
                           for l in range(2)]),
        "a_w_out": f("a_w_out"),
        "kv_ada_w": f("kv_ada_w"),
        "kv_ada_b_bc": _bc(f("kv_ada_b")),
        "kv_norm_g_pp": _pp(f("kv_norm_g"), 8),
        "w_kv": f("w_kv"),
        "k_norm_g_pp": _pp(f("k_norm_g"), 1),
        "b_ada_w": f("b_ada_w"),
        "b_ada_b_bc": np.stack([_bc(f("b_ada_b")[l]) for l in range(2)]),
        "b_norm_g_pp": np.stack([_pp(f("b_norm_g")[l], 8) for l in range(2)]),
        "b_w_in": f("b_w_in"),
        "b_q_norm_g_pp": np.stack([_pp(f("b_q_norm_g")[l], 1) for l in range(2)]),
        "b_w_out": f("b_w_out"),
    }
    x = f("x")
    c = f("c")
    maps = []
    for b in range(8):
        m = dict(shared)
        m["x"] = np.ascontiguousarray(x[b])
        m["cT"] = _pp(c[b], 8)
        maps.append(m)
    return maps


_NC_CACHE = {}


def kernel(**inputs):
    key = "full"
    if key not in _NC_CACHE:
        _NC_CACHE[key] = build_program()
    nc = _NC_CACHE[key]
    in_maps = make_in_maps(inputs)
    res = run_bass_kernel_spmd(nc, in_maps, core_ids=list(range(8)))
    return np.stack([np.asarray(r["out"], np.float32) for r in res.results], axis=0)
```

```python
import os as _os
import numpy as np
from contextlib import ExitStack
import concourse.bass as bass
import concourse.mybir as mybir
from concourse.bass_utils import run_bass_kernel_spmd

F32 = mybir.dt.float32
BF16 = mybir.dt.bfloat16
AF = mybir.ActivationFunctionType
ALU = mybir.AluOpType

S = 4096
D = 1024
NT = S // 128
EPS = 1e-6
ARENA_BYTES = 211968
EXP_SHIFT = 4.0
PE_FIX = float(_os.environ.get("MK_PE_FIX", "0.1"))
QRAW = int(_os.environ.get("MK_QRAW", "0"))
STRICT = int(_os.environ.get("MK_STRICT", "2"))
DMA_BW = float(_os.environ.get("MK_DMA_BW", "450e3"))
PRIO_MODE = int(_os.environ.get("MK_PRIO", "1"))
PRIO_W = float(_os.environ.get("MK_PRIO_W", "1000.0"))
SEM_LAT = float(_os.environ.get("MK_LAT", "0.6"))
COMPUTE = ("pe", "act", "dve", "pool")
ISSUERS = ("pe", "act", "dve", "pool", "sp")


class _CostProbe:
    def __init__(self, issuer):
        self.issuer = issuer
        self.t = 0.0
        self.aset = None

    @staticmethod
    def _free(ap):
        n = 1
        for s_ in ap.shape[1:]:
            n *= s_
        return n

    def matmul(self, out, lhsT=None, rhs=None, **kw):
        c = self._free(rhs)
        if c >= 512:
            n = 0.216 * c / 512.0
        elif c >= 256:
            n = 0.110 + (c - 256) * (0.216 - 0.110) / 256.0
        elif c >= 128:
            n = 0.056 + (c - 128) * (0.110 - 0.056) / 128.0
        else:
            n = 0.030 + max(c - 64, 0) * (0.056 - 0.030) / 64.0
        if self.t == 0.0:
            n += PE_FIX
        if rhs.dtype == F32:
            n *= 4
        self.t += n
        return self

    def transpose(self, out, in_, ident):
        self.t += (0.25 if in_.dtype == F32 else 0.06) + (PE_FIX if self.t == 0.0 else 0.0)
        return self

    def activation(self, out=None, in_=None, func=None, **kw):
        self.t += self._free(in_) / 1100.0 + 0.15
        if func in (AF.Gelu_apprx_tanh, AF.Tanh):
            self.aset = "g"
        elif func == AF.Silu:
            self.aset = "s"
        elif func in (AF.Ln, AF.Exp):
            self.aset = "le"
        return self

    def _ew(self, out):
        if self.issuer == "pool":
            self.t += self._free(out) / 450.0 + 0.2
        else:
            self.t += self._free(out) / 800.0 + 0.12
        return self

    def tensor_tensor(self, out=None, **kw):
        return self._ew(out)

    def tensor_scalar(self, out=None, **kw):
        return self._ew(out)

    def scalar_tensor_tensor(self, out=None, **kw):
        return self._ew(out)

    def tensor_copy(self, out=None, **kw):
        return self._ew(out)

    def memset(self, ap, c):
        return self._ew(ap)

    def dma_start(self, out=None, in_=None, **kw):
        esz = 4 if (out.dtype == F32 or in_.dtype == F32) else 2
        n = 1
        for s_ in out.shape:
            n *= s_
        self.t += n * esz / DMA_BW
        return self


class Sched:
    def __init__(self):
        self.ops = []
        self.last_w = {}
        self.readers = {}
        self.chan_last = {}
        self.seg = 0
        self.fixed_segs = set()

    def add(self, issuer, fn, reads=(), writes=(), chan=None, extra_deps=(), est=None, aset=None):
        idx = len(self.ops)
        if est is None:
            if fn is None:
                est = 0.0
            else:
                pr = _CostProbe(issuer)
                fn(pr)
                est = pr.t
                aset = pr.aset
        key = issuer if chan is None else ("dma", chan)
        deps = {}
        for r in reads:
            w = self.last_w.get(r)
            if w is not None:
                deps[w] = True
        for r in writes:
            w = self.last_w.get(r)
            if w is not None:
                deps.setdefault(w, False)
            for ri in self.readers.get(r, ()):
                deps.setdefault(ri, False)
        for j in extra_deps:
            deps[j] = True
        deps.pop(idx, None)
        self.ops.append(dict(idx=idx, issuer=issuer, key=key, fn=fn, deps=deps, flag=chan is not None,
                             est=est, aset=aset, seg=self.seg, pos=0, count=0, waits=[], dma=chan is not None,
                             tag=(list(writes) or ["-"])[0]))
        for r in reads:
            self.readers.setdefault(r, []).append(idx)
        for r in writes:
            self.last_w[r] = idx
            self.readers[r] = []
        if chan is not None:
            self.chan_last[chan] = idx
        return idx

    def barrier(self, marker_fns, keep=(), skip_chans=()):
        self.seg += 1
        self.fixed_segs.add(self.seg)
        marks = []
        for e in COMPUTE:
            marks.append(self.add(e, marker_fns[e], reads=marker_fns.get(e + "_r", ()), writes=marker_fns[e + "_w"], est=0.1))
        deps = marks + [v for c, v in self.chan_last.items() if c not in skip_chans]
        for e in ISSUERS:
            self.add(e, None, extra_deps=deps, est=0.0)
        self.seg += 1
        kept = {}
        for r, w in self.last_w.items():
            if isinstance(r, tuple) and r[0] in keep:
                kept[r] = w
        self.last_w = kept
        self.readers = {}

    def finish(self):
        self.seg += 1
        self.fixed_segs.add(self.seg)
        self.add("sp", None, extra_deps=list(self.chan_last.values()), est=0.0)

    def schedule(self):
        ops = self.ops
        n = len(ops)
        end = [0.0] * n
        startt = [0.0] * n
        free = {e: 0.0 for e in ISSUERS}
        act_set = [None]
        pipe = [0.0]
        order = []
        LAT = SEM_LAT

        def ready_time(op):
            t = 0.0
            for j in op["deps"]:
                pj = ops[j]
                lat = 0.0 if (pj["key"] == op["key"] and not pj["dma"]) else LAT
                if end[j] + lat > t:
                    t = end[j] + lat
            return t

        def place(op, st):
            i = op["idx"]
            dur = op["est"]
            if op["issuer"] == "act" and op["aset"] is not None and op["aset"] != act_set[0]:
                dur += 1.3
                act_set[0] = op["aset"]
            startt[i] = st
            if op["dma"]:
                issue = 1.0 if op["issuer"] == "pool" else 0.08
                free[op["issuer"]] = st + issue
                xs = max(st + issue + 1.5, pipe[0])
                end[i] = xs + dur + 0.5
                pipe[0] = xs + dur
            else:
                free[op["issuer"]] = st + dur
                end[i] = st + dur
            order.append(i)

        segs = {}
        for op in ops:
            segs.setdefault(op["seg"], []).append(op["idx"])
        for sg in sorted(segs):
            idxs = segs[sg]
            if sg in self.fixed_segs:
                for i in idxs:
                    op = ops[i]
                    place(op, max(free[op["issuer"]], ready_time(op)))
                continue
            inseg = set(idxs)
            indeg = {}
            succ = {}
            for i in idxs:
                c = 0
                for j in ops[i]["deps"]:
                    if j in inseg:
                        c += 1
                        succ.setdefault(j, []).append(i)
                indeg[i] = c
            prio = {}
            if PRIO_MODE:
                bl = {}
                for i in reversed(idxs):
                    m_ = 0.0
                    for k in succ.get(i, ()):
                        v = bl[k] + LAT
                        if v > m_:
                            m_ = v
                    bl[i] = ops[i]["est"] + m_
                tot = max(bl.values()) if bl else 1.0
                for i in idxs:
                    prio[i] = i - PRIO_W * bl[i]
            else:
                for i in idxs:
                    prio[i] = i
            ready = {e: [] for e in ISSUERS}
            rt = {}
            for i in idxs:
                if indeg[i] == 0:
                    rt[i] = ready_time(ops[i])
                    ready[ops[i]["issuer"]].append(i)
            left = len(idxs)
            while left:
                best = None
                for e in ISSUERS:
                    rl = ready[e]
                    if not rl:
                        continue
                    now = free[e]
                    avail = [i for i in rl if rt[i] <= now]
                    if avail:
                        pick = min(avail, key=lambda i: prio[i])
                        if e == "act" and rt[pick] > now - 4.0:
                            same = [i for i in avail if ops[i]["aset"] in (None, act_set[0])]
                            if same:
                                pick = min(same, key=lambda i: prio[i])
                        st = now
                    else:
                        pick = min(rl, key=lambda i: (rt[i], prio[i]))
                        st = rt[pick]
                    if best is None or (st, pick) < best[:2]:
                        best = (st, pick, e)
                st, pick, e = best
                ready[e].remove(pick)
                place(ops[pick], st)
                left -= 1
                for k in succ.get(pick, ()):
                    indeg[k] -= 1
                    if indeg[k] == 0:
                        rt[k] = ready_time(ops[k])
                        ready[ops[k]["issuer"]].append(k)
        self.order = order
        self.sim_endt = end
        self.sim_end = max(end) if end else 0.0
        self.sim_start = startt
        return self.sim_end

    def report(self):
        segs = {}
        for op in self.ops:
            i = op["idx"]
            d = segs.setdefault(op["seg"], dict(t0=1e18, t1=0.0, busy={e: 0.0 for e in ISSUERS}, n=0))
            d["t0"] = min(d["t0"], self.sim_start[i])
            if not op["dma"]:
                d["t1"] = max(d["t1"], self.sim_endt[i])
                d["busy"][op["issuer"]] += op["est"]
            d["n"] += 1
        for sg in sorted(segs):
            d = segs[sg]
            if sg in self.fixed_segs:
                continue
            dur = d["t1"] - d["t0"]
            print("seg %3d t0=%8.1f dur=%8.1f n=%5d busy%%: %s" % (
                sg, d["t0"], dur, d["n"],
                " ".join("%s=%3.0f" % (e, 100 * d["busy"][e] / max(dur, 1e-9)) for e in COMPUTE)), flush=True)

    def resolve(self):
        ops = self.ops
        self.schedule()
        pos = {e: 0 for e in ISSUERS}
        for i in self.order:
            op = ops[i]
            op["pos"] = pos[op["issuer"]]
            pos[op["issuer"]] += 1
        need = {}
        for i in self.order:
            op = ops[i]
            lst = []
            for j, raw in op["deps"].items():
                pj = ops[j]
                if pj["fn"] is None:
                    continue
                if pj["key"] == op["key"] and pj["key"] in COMPUTE:
                    if STRICT == 0 and (not raw or op["pos"] - pj["pos"] >= 4):
                        continue
                    if STRICT == 1 and not raw and op["pos"] - pj["pos"] >= 64:
                        continue
                pj["flag"] = True
                lst.append(j)
            need[i] = lst
        counts = {}
        for i in self.order:
            op = ops[i]
            if op["flag"] and op["fn"] is not None:
                k = op["key"]
                counts[k] = counts.get(k, 0) + (16 if isinstance(k, tuple) else 1)
                op["count"] = counts[k]
        waited = {e: {} for e in ISSUERS}
        for i in self.order:
            op = ops[i]
            w = {}
            for j in need[i]:
                pj = ops[j]
                w[pj["key"]] = max(w.get(pj["key"], 0), pj["count"])
            wd = waited[op["issuer"]]
            for k, c in w.items():
                if wd.get(k, 0) < c:
                    wd[k] = c
                    op["waits"].append((k, c))
        self.keys = list(counts.keys())

    def emit(self, issuer, eng, sems):
        for i in self.order:
            op = self.ops[i]
            if op["issuer"] != issuer:
                continue
            for k, c in op["waits"]:
                eng.wait_ge(sems[k], c)
            if op["fn"] is None:
                continue
            ins = op["fn"](eng)
            if op["flag"]:
                ins.then_inc(sems[op["key"]], 16 if isinstance(op["key"], tuple) else 1)


class Arena:
    def __init__(self, ap):
        self.ap = ap
        self.off = 0

    def reset(self, off=0):
        self.off = off

    def take(self, dtype, shape):
        n = 1
        for s_ in shape[1:]:
            n *= s_
        esz = 4 if dtype == F32 else 2
        nbytes = (n * esz + 63) // 64 * 64
        assert self.off + nbytes <= ARENA_BYTES, (self.off, nbytes)
        o32 = self.off // 4
        v = self.ap[:, o32:o32 + nbytes // 4]
        self.off += nbytes
        if dtype != F32:
            v = v.bitcast(dtype)
        v = v[:, 0:n]
        if len(shape) == 3:
            v = v.rearrange("p (a b) -> p a b", b=shape[2])
        elif len(shape) == 4:
            v = v.rearrange("p (a b c) -> p a b c", b=shape[2], c=shape[3])
        return v


def build_program(n_a=2, do_kv=True, n_b=2):
    nc = bass.Bass("TRN2", target_bir_lowering=False)
    sch = Sched()

    def din(name, shape):
        return nc.dram_tensor(name, list(shape), F32, kind="ExternalInput").ap()

    x_d = din("x", [S, D])
    c_d = din("cT", [128, 8])
    ident_d = din("ident", [128, 128])
    triu_d = din("triu", [128, 128])
    etab_d = din("etab", [128, 12, 512])
    a_ada_w = din("a_ada_w", [2, D, 3072])
    a_ada_b = din("a_ada_b_bc", [2, 128, 3072])
    a_norm_g = din("a_norm_g_pp", [2, 128, 8])
    a_w_in = din("a_w_in", [2, D, 6144])
    a_sgu = din("a_sgu_bc", [2, 128, 2048])
    a_wsT = din("a_wsT", [2, 128, 8, 128])
    a_bsb = din("a_bsb", [2, 128, 8, 128])
    a_w_out = din("a_w_out", [2, 2048, D])
    kv_ada_w = din("kv_ada_w", [D, 2048])
    kv_ada_b = din("kv_ada_b_bc", [128, 2048])
    kv_norm_g = din("kv_norm_g_pp", [128, 8])
    w_kv = din("w_kv", [D, 1536])
    k_norm_g = din("k_norm_g_pp", [128, 1])
    b_ada_w = din("b_ada_w", [2, D, 3072])
    b_ada_b = din("b_ada_b_bc", [2, 128, 3072])
    b_norm_g = din("b_norm_g_pp", [2, 128, 8])
    b_w_in = din("b_w_in", [2, D, 4096])
    b_qg = din("b_q_norm_g_pp", [2, 128, 1])
    b_w_out = din("b_w_out", [2, D, D])
    out_d = nc.dram_tensor("out", [S, D], F32, kind="ExternalOutput").ap()
    qs_d = nc.dram_tensor("qs", [24, 16, 128, 256], BF16).ap()
    szs_d = nc.dram_tensor("szs", [16, 128, 8, 256], BF16).ap()
    gate_d = nc.dram_tensor("gate_sc", [128, 1024], F32).ap()

    es = ExitStack()
    arena_t = es.enter_context(nc.sbuf_tensor("arena", [128, ARENA_BYTES // 4], F32))
    psum_t = es.enter_context(nc.psum_tensor("psum", [128, 4096], F32))
    ar = Arena(arena_t[:])
    ps = psum_t[:]

    def bank(b):
        return ps[:, 512 * b:512 * (b + 1)]

    def bank_bf(b):
        return bank(b).bitcast(BF16)

    ident_bf = ar.take(BF16, [128, 128])
    ident_f = ar.take(F32, [128, 128])
    ones_bf = ar.take(BF16, [128, 128])
    triu_f = ar.take(F32, [128, 128])
    c_col = ar.take(F32, [128, 8])
    s_col = ar.take(F32, [128, 8])
    gs_pp = ar.take(F32, [128, 8])
    sh_pp = ar.take(F32, [128, 8])
    ng_pp = ar.take(F32, [128, 8])
    qg_pp = ar.take(F32, [128, 1])
    kg_pp = ar.take(F32, [128, 1])
    stat = ar.take(F32, [128, 64])
    scr = ar.take(F32, [128, 16])
    PERSIST_END = ar.off

    statn = [0]
    uch = [0]

    def uchan():
        uch[0] += 1
        return "u%d" % uch[0]

    def stat_slot():
        i = statn[0] % 64
        statn[0] += 1
        return stat[:, i:i + 1], ("stat", i)

    def mk_markers():
        return {
            "pe": lambda e: e.matmul(bank(7)[:, 0:2], lhsT=ident_bf[:, 0:128], rhs=ident_bf[:, 0:2],
                                     start=True, stop=True),
            "pe_w": [("ps", 7)],
            "pe_r": ["ident_bf"],
            "act_r": ["scr_a"],
            "act": lambda e: e.activation(out=scr[:, 0:1], in_=scr[:, 1:2], func=AF.Copy),
            "act_w": ["scr_a"],
            "dve": lambda e: e.memset(scr[:, 2:3], 0.0),
            "dve_w": ["scr_d"],
            "pool": lambda e: e.memset(scr[:, 4:5], 0.0),
            "pool_w": ["scr_p"],
        }

    def barrier(keep=(), skip_chans=()):
        sch.barrier(mk_markers(), keep=keep, skip_chans=skip_chans)
        uch[0] = 0

    sch.add("pool", lambda e: e.dma_start(out=ident_bf, in_=ident_d[:, :]), writes=["ident_bf"], chan="pconst")
    sch.add("sp", lambda e: e.dma_start(out=ident_f, in_=ident_d[:, :]), writes=["ident_f"], chan=uchan())
    sch.add("sp", lambda e: e.dma_start(out=triu_f, in_=triu_d[:, :]), writes=["triu_f"], chan=uchan())
    sch.add("sp", lambda e: e.dma_start(out=c_col, in_=c_d[:, :]), writes=["c_col"], chan=uchan())
    sch.add("dve", lambda e: e.memset(ones_bf, 1.0), writes=["ones_bf"])
    sch.add("dve", lambda e: e.memset(scr, 0.0), writes=["scr_a", "scr_d", "scr_p"])
    sch.add("act", lambda e: e.activation(out=s_col, in_=c_col, func=AF.Silu), reads=["c_col"], writes=["s_col"])

    def ada_phase(w_ap, b_ap, ncols, work_off, cw=512):
        ar.reset(work_off)
        ada_bc = ar.take(F32, [128, 3072])
        s_rep = ar.take(F32, [128, 8, 128])
        awb = [ar.take(F32, [128, 8, cw]) for _ in range(2)]
        for kc in range(8):
            sch.add("dve", lambda e, kc=kc: e.tensor_copy(out=s_rep[:, kc, :],
                                                           in_=s_col[:, kc:kc + 1].to_broadcast([128, 128])),
                    reads=["s_col"], writes=["s_rep"])
        wv = w_ap.rearrange("(c p) n -> p c n", p=128)
        nch = ncols // cw
        sch.add("sp", lambda e: e.dma_start(out=ada_bc[:, 0:ncols], in_=b_ap), writes=["ada_bc"], chan=uchan())
        for j in range(nch):
            b = j % 2
            for h in range(2):
                sch.add("sp", lambda e, j=j, b=b, h=h: e.dma_start(
                    out=awb[b][:, 4 * h:4 * h + 4, :], in_=wv[:, 4 * h:4 * h + 4, cw * j:cw * (j + 1)]),
                    writes=[("aw", b, h)], chan=("aw", b, h))
            pb = 4 + (j % 2)

            def mm(e, j=j, b=b, pb=pb):
                for kc in range(8):
                    ins = e.matmul(bank(pb)[:, 0:cw], lhsT=s_rep[:, kc, :], rhs=awb[b][:, kc, :],
                                   start=(kc == 0), stop=(kc == 7))
                return ins
            sch.add("pe", mm, reads=[("aw", b, 0), ("aw", b, 1), "s_rep"], writes=[("ps", pb)])
            sch.add("dve", lambda e, j=j, pb=pb: e.tensor_tensor(
                out=ada_bc[:, cw * j:cw * (j + 1)], in0=bank(pb)[:, 0:cw], in1=ada_bc[:, cw * j:cw * (j + 1)],
                op=ALU.add), reads=[("ps", pb), "ada_bc"], writes=["ada_bc"])
        return ada_bc

    def ada_pp(norm_ap, ada_bc):
        sch.add("sp", lambda e: e.dma_start(out=ng_pp, in_=norm_ap), writes=["ng_pp"], chan=uchan())
        for half in range(2):
            for q in range(2):
                pb = 4 + q

                def tr(e, half=half, q=q, pb=pb):
                    for i in range(4):
                        col = 1024 * half + 128 * (4 * q + i)
                        ins = e.transpose(bank(pb)[:, 128 * i:128 * (i + 1)], ada_bc[:, col:col + 128], ident_f)
                    return ins
                sch.add("pe", tr, reads=["ada_bc", "ident_f"], writes=[("ps", pb)])
                dst = (sh_pp if half == 0 else gs_pp)[:, 4 * q:4 * q + 4]
                src = bank(pb).rearrange("p (a b) -> p a b", b=128)[:, :, 0]
                sch.add("dve", lambda e, dst=dst, src=src: e.tensor_copy(out=dst, in_=src),
                        reads=[("ps", pb)], writes=["sh_pp" if half == 0 else "gs_pp"])
        sch.add("dve", lambda e: e.scalar_tensor_tensor(out=gs_pp, in0=gs_pp, scalar=1.0, in1=ng_pp,
                                                         op0=ALU.add, op1=ALU.mult),
                reads=["gs_pp", "ng_pp"], writes=["gs_pp"])

    def front(src_d, t, xbuf, xn, hT, pb, xres_name, xn_name, hT_name):
        sch.add("sp", lambda e: e.dma_start(out=xbuf, in_=src_d[128 * t:128 * (t + 1), :]),
                writes=[xres_name], chan=xres_name)
        ss, ssn = stat_slot()
        r1, r1n = stat_slot()
        sch.add("act", lambda e: e.activation(out=xn, in_=xbuf, func=AF.Square, accum_out=ss),
                reads=[xres_name], writes=[xn_name, ssn])
        sch.add("act", lambda e: e.activation(out=r1, in_=ss, func=AF.Ln, scale=1.0 / D, bias=eps_col),
                reads=[ssn, "eps_col"], writes=[r1n])
        sch.add("act", lambda e: e.activation(out=r1, in_=r1, func=AF.Exp, scale=-0.5),
                reads=[r1n], writes=[r1n])
        sch.add("dve", lambda e: e.tensor_scalar(out=xn, in0=xbuf, scalar1=r1, scalar2=None, op0=ALU.mult),
                reads=[xres_name, r1n], writes=[xn_name])

        def tr(e):
            pbf = bank_bf(pb)
            for kc in range(8):
                ins = e.transpose(pbf[:, 128 * kc:128 * (kc + 1)], xn[:, 128 * kc:128 * (kc + 1)], ident_bf)
            return ins
        sch.add("pe", tr, reads=[xn_name, "ident_bf"], writes=[("ps", pb)])
        for kc in range(8):
            sch.add("dve", lambda e, kc=kc: e.tensor_scalar(
                out=hT[:, kc, :], in0=bank_bf(pb)[:, 128 * kc:128 * (kc + 1)],
                scalar1=gs_pp[:, kc:kc + 1], scalar2=sh_pp[:, kc:kc + 1], op0=ALU.mult, op1=ALU.add),
                reads=[("ps", pb), "gs_pp", "sh_pp"], writes=[hT_name])

    eps_col = ar.take(F32, [128, 1]) if False else None

    ar.reset(PERSIST_END)
    eps_col = ar.take(F32, [128, 1])
    one_col = ar.take(F32, [128, 1])
    nshift_col = ar.take(F32, [128, 1])
    gate_p = ar.take(F32, [128, 1024])
    PERSIST_END = ar.off
    sch.add("dve", lambda e: e.memset(eps_col, EPS), writes=["eps_col"])
    sch.add("dve", lambda e: e.memset(one_col, 1.0), writes=["one_col"])
    sch.add("dve", lambda e: e.memset(nshift_col, -EXP_SHIFT), writes=["nshift_col"])

    def gmlp_layer(l):
        src_d = x_d if l == 0 else out_d
        barrier()
        ar.reset(PERSIST_END)
        Win = ar.take(BF16, [128, 8, 6144])
        Wout = ar.take(BF16, [128, 16, 1024])
        WsT = ar.take(BF16, [128, 8, 128])
        bsb = ar.take(F32, [128, 8, 128])
        sgu = ar.take(F32, [128, 2048])
        WORK = ar.off
        ada_bc = ada_phase(a_ada_w[l], a_ada_b[l], 3072, WORK)
        ada_pp(a_norm_g[l], ada_bc)
        ada_last = [sch.chan_last[("aw", b_, h_)] for b_ in range(2) for h_ in range(2)]
        wv = a_w_in[l].rearrange("(c p) n -> p c n", p=128)
        for h in (1, 0, 2):
            for kc in range(8):
                sch.add("pool", lambda e, kc=kc, h=h: e.dma_start(
                    out=Win[:, kc, 2048 * h:2048 * (h + 1)], in_=wv[:, kc, 2048 * h:2048 * (h + 1)]),
                    writes=[("Win", h, kc)], chan="win%d" % h, extra_deps=ada_last)
        wo = a_w_out[l].rearrange("(c p) n -> p c n", p=128)
        for ec in range(0, 16, 2):
            sch.add("pool", lambda e, ec=ec: e.dma_start(out=Wout[:, ec:ec + 2, :], in_=wo[:, ec:ec + 2, :]),
                    writes=[("Wout", ec), ("Wout", ec + 1)], chan="wout", extra_deps=ada_last)
        sch.add("sp", lambda e: e.dma_start(out=bsb, in_=a_bsb[l]), writes=["bsb"], chan=uchan())
        sch.add("sp", lambda e: e.dma_start(out=sgu, in_=a_sgu[l]), writes=["sgu"], chan=uchan())
        wst_f = ar.take(F32, [128, 8, 128])
        sch.add("sp", lambda e: e.dma_start(out=wst_f, in_=a_wsT[l]), writes=["wst_f"], chan=uchan())
        sch.add("dve", lambda e: e.tensor_tensor(
            out=WsT, in0=wst_f, in1=triu_f.unsqueeze(1).to_broadcast([128, 8, 128]), op=ALU.mult),
            reads=["wst_f", "triu_f"], writes=["WsT"])
        sch.add("dve", lambda e: e.tensor_scalar(out=gate_p, in0=ada_bc[:, 2048:3072], scalar1=0.5,
                                                  scalar2=None, op0=ALU.mult), reads=["ada_bc"], writes=["gate_p"])
        barrier(keep=("Win", "Wout"), skip_chans=("win0", "win1", "win2", "wout"))
        for ec in range(16):
            sch.add("pool" if ec % 2 else "dve", lambda e, ec=ec: e.tensor_tensor(
                out=Wout[:, ec, :], in0=Wout[:, ec, :], in1=gate_p, op=ALU.mult),
                reads=[("Wout", k_) for k_ in range(16)], writes=[("WoutF", ec)])
        WoutF = [("WoutF", ec) for ec in range(16)]
        WinV = [("Win", 1, kc) for kc in range(8)]
        WinU = [("Win", 0, kc) for kc in range(8)]
        WinZ = [("Win", 2, kc) for kc in range(8)]
        ar.reset(WORK)
        NXB = 3
        xb = [ar.take(F32, [128, 1024]) for _ in range(NXB)]
        xn = ar.take(BF16, [128, 1024])
        hT = [ar.take(BF16, [128, 8, 128]) for _ in range(2)]
        gv = ar.take(F32, [128, 2048])
        vn = [ar.take(BF16, [128, 2048]) for _ in range(2)]
        gu = ar.take(BF16, [128, 2048])
        szb = [ar.take(F32, [128, 512]) for _ in range(2)]
        gg = ar.take(BF16, [128, 2048])
        t3 = [ar.take(F32, [128, 512]) for _ in range(2)]
        yT = [ar.take(BF16, [128, 16, 128]) for _ in range(2)]
        rot = [0]

        def nextA():
            b = rot[0] % 4
            rot[0] += 1
            return b

        def do_front(t):
            b = t % NXB
            front(src_d, t, xb[b], xn, hT[t % 2], nextA(), ("xb", b), "xn", ("hT", t % 2))

        do_front(0)
        for t in range(NT):
            b = t % NXB
            hTt = hT[t % 2]
            hTn = ("hT", t % 2)
            vnt = vn[t % 2]
            vnn = ("vn", t % 2)
            ssv, ssvn = stat_slot()
            for j in range(4):
                pb = nextA()

                def mm(e, j=j, pb=pb, hTt=hTt):
                    for kc in range(8):
                        ins = e.matmul(bank(pb), lhsT=hTt[:, kc, :], rhs=Win[:, kc, 2048 + 512 * j:2048 + 512 * (j + 1)],
                                       start=(kc == 0), stop=(kc == 7))
                    return ins
                sch.add("pe", mm, reads=[hTn] + WinV, writes=[("ps", pb)])
                sch.add("act", lambda e, j=j, pb=pb: e.activation(
                    out=gv[:, 512 * j:512 * (j + 1)], in_=bank(pb), func=AF.Gelu_apprx_tanh),
                    reads=[("ps", pb)], writes=[("gv", j)])
            sch.add("act", lambda e, ssv=ssv, vnt=vnt: e.activation(out=vnt, in_=gv, func=AF.Square, accum_out=ssv),
                    reads=[("gv", j) for j in range(4)], writes=[vnn, ssvn])
            rv, rvn = stat_slot()
            sch.add("act", lambda e, rv=rv, ssv=ssv: e.activation(out=rv, in_=ssv, func=AF.Ln, scale=1.0 / 2048, bias=eps_col),
                    reads=[ssvn, "eps_col"], writes=[rvn])
            sch.add("act", lambda e, rv=rv: e.activation(out=rv, in_=rv, func=AF.Exp, scale=-0.5),
                    reads=[rvn], writes=[rvn])
            sch.add("dve", lambda e, rv=rv, vnt=vnt: e.scalar_tensor_tensor(
                out=vnt, in0=gv, scalar=rv, in1=sgu, op0=ALU.mult, op1=ALU.mult),
                reads=[("gv", j) for j in range(4)] + [rvn, "sgu"], writes=[vnn])
            if t + 1 < NT:
                do_front(t + 1)
            for j in range(4):
                pb = nextA()

                def mm(e, j=j, pb=pb, hTt=hTt):
                    for kc in range(8):
                        ins = e.matmul(bank(pb), lhsT=hTt[:, kc, :], rhs=Win[:, kc, 512 * j:512 * (j + 1)],
                                       start=(kc == 0), stop=(kc == 7))
                    return ins
                sch.add("pe", mm, reads=[hTn] + WinU, writes=[("ps", pb)])
                sch.add("act", lambda e, j=j, pb=pb: e.activation(
                    out=gu[:, 512 * j:512 * (j + 1)], in_=bank(pb), func=AF.Gelu_apprx_tanh),
                    reads=[("ps", pb)], writes=[("gu", j)])
            mix_banks = []
            for j in range(4):
                pb = nextA()

                def mm(e, j=j, pb=pb, hTt=hTt):
                    for kc in range(8):
                        ins = e.matmul(bank(pb), lhsT=hTt[:, kc, :], rhs=Win[:, kc, 4096 + 512 * j:4096 + 512 * (j + 1)],
                                       start=(kc == 0), stop=(kc == 7))
                    return ins
                sch.add("pe", mm, reads=[hTn] + WinZ, writes=[("ps", pb)])
                sb = j % 2
                sch.add("act", lambda e, sb=sb, pb=pb: e.activation(out=szb[sb], in_=bank(pb), func=AF.Tanh, scale=0.5),
                        reads=[("ps", pb)], writes=[("szb", sb)])
                sch.add("dve", lambda e, sb=sb, pb=pb: e.scalar_tensor_tensor(
                    out=szb[sb], in0=szb[sb], scalar=1.0, in1=bank(pb), op0=ALU.add, op1=ALU.mult),
                    reads=[("ps", pb), ("szb", sb)], writes=[("szb", sb)])
                sch.add("pool", lambda e, j=j, sb=sb: e.tensor_tensor(
                    out=gg[:, 512 * j:512 * (j + 1)], in0=gu[:, 512 * j:512 * (j + 1)], in1=szb[sb], op=ALU.mult),
                    reads=[("gu", j), ("szb", sb)], writes=[("gg", j)])
            yTt = yT[t % 2]
            yTn = ("yT", t % 2)
            for q in range(4):
                pm = 4 + (q % 2)
                pg = 6 + (q % 2)

                def mix(e, q=q, pm=pm, vnt=vnt):
                    for i in range(4):
                        ec = 4 * q + i
                        ins = e.matmul(bank(pm)[:, 128 * i:128 * (i + 1)], lhsT=vnt[:, 128 * ec:128 * (ec + 1)],
                                       rhs=WsT[:, ec // 2, :], start=True, stop=True)
                    return ins
                sch.add("pe", mix, reads=[vnn, "WsT"], writes=[("ps", pm)])

                def trg(e, q=q, pg=pg):
                    pbf = bank_bf(pg)
                    for i in range(4):
                        ec = 4 * q + i
                        ins = e.transpose(pbf[:, 128 * i:128 * (i + 1)], gg[:, 128 * ec:128 * (ec + 1)], ident_bf)
                    return ins
                sch.add("pe", trg, reads=[("gg", q), "ident_bf"], writes=[("ps", pg)])
                tb = q % 2
                bs_v = bsb[:, 2 * q:2 * q + 2, :].unsqueeze(2).to_broadcast([128, 2, 2, 128])
                sch.add("dve", lambda e, pm=pm, tb=tb, bs_v=bs_v: e.tensor_tensor(
                    out=t3[tb].rearrange("p (a b c) -> p a b c", a=2, b=2), in0=bank(pm).rearrange("p (a b c) -> p a b c", a=2, b=2),
                    in1=bs_v, op=ALU.add),
                    reads=[("ps", pm), "bsb"], writes=[("t3", tb)])
                sch.add("dve", lambda e, q=q, pg=pg, tb=tb, yTt=yTt: e.tensor_tensor(
                    out=yTt[:, 4 * q:4 * q + 4, :], in0=bank_bf(pg)[:, 0:512].rearrange("p (a b) -> p a b", b=128),
                    in1=t3[tb].rearrange("p (a b) -> p a b", b=128), op=ALU.mult),
                    reads=[("ps", pg), ("t3", tb)], writes=[yTn])
            for h in range(2):
                pb = nextA()

                def mo(e, h=h, pb=pb, yTt=yTt):
                    for ec in range(16):
                        ins = e.matmul(bank(pb), lhsT=yTt[:, ec, :], rhs=Wout[:, ec, 512 * h:512 * (h + 1)],
                                       start=(ec == 0), stop=(ec == 15))
                    return ins
                sch.add("pe", mo, reads=[yTn] + WoutF, writes=[("ps", pb)])
                sch.add("dve", lambda e, h=h, pb=pb, b=b: e.tensor_tensor(
                    out=xb[b][:, 512 * h:512 * (h + 1)], in0=bank(pb), in1=xb[b][:, 512 * h:512 * (h + 1)], op=ALU.add),
                    reads=[("ps", pb), ("xb", b)], writes=[("xb", b)])
            sch.add("sp", lambda e, t=t, b=b: e.dma_start(out=out_d[128 * t:128 * (t + 1), :], in_=xb[b]),
                    reads=[("xb", b)], writes=[("outrow", t)], chan=("st", b))

    for l in range(n_a):
        gmlp_layer(l)

    DIL = (1, 4, 16)
    STW = 256
    NST = S // STW
    kvs = {}

    STA = 512
    NSTA = S // STA

    def dil_view(ap2d, d):
        return ap2d.rearrange("p (q r) -> p r q", r=d)

    def norm_to(pb, pb2, sq, rs, gcol, pieces, nm, qraw=None, dst_res=None):
        if qraw is None:
            sch.add("act", lambda e: e.activation(out=sq, in_=bank(pb), func=AF.Square),
                    reads=[("ps", pb)], writes=[nm + "sq"])
            src = bank(pb)
            src_res = ("ps", pb)
        else:
            sch.add("dve", lambda e: e.tensor_copy(out=qraw, in_=bank(pb)), reads=[("ps", pb)], writes=[nm + "raw"])
            sch.add("pool", lambda e: e.tensor_tensor(out=sq, in0=qraw, in1=qraw, op=ALU.mult),
                    reads=[nm + "raw"], writes=[nm + "sq"])
            src = qraw
            src_res = nm + "raw"
        sch.add("pe", lambda e: e.matmul(bank(pb2), lhsT=ones_bf, rhs=sq, start=True, stop=True),
                reads=[nm + "sq", "ones_bf"], writes=[("ps", pb2)])
        sch.add("act", lambda e: e.activation(out=rs, in_=bank(pb2), func=AF.Ln, scale=1.0 / 128, bias=eps_col),
                reads=[("ps", pb2), "eps_col"], writes=[nm + "rs"])
        sch.add("act", lambda e: e.activation(out=rs, in_=rs, func=AF.Exp, scale=-0.5),
                reads=[nm + "rs"], writes=[nm + "rs"])
        for (dst_view, c0, w, d) in pieces:
            sch.add("dve", lambda e, dst_view=dst_view, c0=c0, w=w, d=d: e.scalar_tensor_tensor(
                out=dst_view, in0=dil_view(src[:, c0:c0 + w], d), scalar=gcol, in1=dil_view(rs[:, c0:c0 + w], d),
                op0=ALU.mult, op1=ALU.mult), reads=[src_res, nm + "rs", "gcol"],
                writes=[nm + "dst"] + ([dst_res] if dst_res is not None else []))

    def kv_phase():
        barrier()
        ar.reset(PERSIST_END - 4096)
        KT = ar.take(BF16, [128, 6, 4096])
        Vst = ar.take(BF16, [128, 6, 32, 128])
        kvs.update(KT=KT, Vst=Vst, END=ar.off)
        Wkv = ar.take(BF16, [128, 8, 1536])
        VT = ar.take(BF16, [128, 6, 4096])
        VT_OFF = ar.off - 6 * 4096 * 2
        WORK = ar.off
        wv = w_kv.rearrange("(c p) n -> p c n", p=128)
        for kc in range(8):
            sch.add("pool", lambda e, kc=kc: e.dma_start(out=Wkv[:, kc, :], in_=wv[:, kc, :]),
                    writes=[("Wkv", kc)], chan="win")
        sch.add("sp", lambda e: e.dma_start(out=kg_pp, in_=k_norm_g[:, :]), writes=["kg_pp"], chan=uchan())
        ada_bc = ada_phase(kv_ada_w, kv_ada_b, 2048, VT_OFF)
        ada_pp(kv_norm_g[:, :], ada_bc)
        barrier(keep=("Wkv",), skip_chans=("win",))
        ar.reset(WORK)
        xb = [ar.take(F32, [128, 1024]) for _ in range(2)]
        xn = ar.take(BF16, [128, 1024])
        sq = [ar.take(BF16, [128, STA]) for _ in range(2)]
        rs = [ar.take(F32, [128, STA]) for _ in range(2)]
        NHK = 2 if ar.off + 2 * 8192 <= ARENA_BYTES else 1
        hTs = [ar.take(BF16, [128, 8, STA]) for _ in range(NHK)]
        if _os.environ.get("MK_DEBUG"):
            print("KV arena used", ar.off, "NHK", NHK, flush=True)
        cnt = [0]
        fb = [0]
        for m in range(NSTA):
            hT = hTs[m % NHK]
            hb = m % NHK
            for tt in range(4):
                t = 4 * m + tt
                b = t % 2
                front(out_d, t, xb[b], xn, hT[:, :, 128 * tt:128 * (tt + 1)], 6 + (fb[0] % 2), ("xb", b), "xn", ("hT", hb, tt))
                fb[0] += 1
            for j in range(12):
                k = cnt[0] % 2
                pb = cnt[0] % 4
                cnt[0] += 1
                pb2 = 4 + k
                g = (j % 6) // 2
                d = DIL[g]

                def mm(e, j=j, pb=pb, hT=hT):
                    for kc in range(8):
                        ins = e.matmul(bank(pb), lhsT=Wkv[:, kc, 128 * j:128 * (j + 1)], rhs=hT[:, kc, :],
                                       start=(kc == 0), stop=(kc == 7))
                    return ins
                sch.add("pe", mm, reads=[("hT", hb, i_) for i_ in range(4)] + [("Wkv", kc_) for kc_ in range(8)],
                        writes=[("ps", pb)])
                q0 = STA * m // d
                if j < 6:
                    dst = KT[:, j, :].rearrange("p (r q) -> p r q", r=d)[:, :, q0:q0 + STA // d]
                    norm_to(pb, pb2, sq[k], rs[k], kg_pp, [(dst, 0, STA, d)], "k%d" % k)
                else:
                    dst = VT[:, j - 6, :].rearrange("p (r q) -> p r q", r=d)[:, :, q0:q0 + STA // d]
                    if j % 2:
                        sch.add("act", lambda e, pb=pb, dst=dst, d=d: e.activation(
                            out=dst, in_=dil_view(bank(pb), d), func=AF.Copy),
                            reads=[("ps", pb)], writes=["VT"])
                    else:
                        sch.add("dve", lambda e, pb=pb, dst=dst, d=d: e.tensor_copy(
                            out=dst, in_=dil_view(bank(pb), d)),
                            reads=[("ps", pb)], writes=["VT"])
        for j in range(6):
            for q in range(4):
                pb = q % 4

                def tr(e, j=j, q=q, pb=pb):
                    for i in range(8):
                        blk = 8 * q + i
                        ins = e.transpose(bank_bf(pb)[:, 128 * i:128 * (i + 1)], VT[:, j, 128 * blk:128 * (blk + 1)], ident_bf)
                    return ins
                sch.add("pe", tr, reads=["VT"], writes=[("ps", pb)])
                sch.add("dve" if q % 2 else "act", (lambda e, j=j, q=q, pb=pb: e.tensor_copy(
                    out=Vst[:, j, 8 * q:8 * q + 8, :], in_=bank_bf(pb).rearrange("p (a b) -> p a b", b=128))) if q % 2 else
                    (lambda e, j=j, q=q, pb=pb: e.activation(
                        out=Vst[:, j, 8 * q:8 * q + 8, :], in_=bank_bf(pb).rearrange("p (a b) -> p a b", b=128), func=AF.Copy)),
                    reads=[("ps", pb)], writes=[("Vst", j, q)])

    def attn_layer(l):
        KT, Vst = kvs["KT"], kvs["Vst"]
        barrier()
        ar.reset(kvs["END"])
        WQ_OFF = ar.off
        Wq = ar.take(BF16, [128, 8, 4096])
        WORK = ar.off
        ada_bc = ada_phase(b_ada_w[l], b_ada_b[l], 3072, WORK, cw=256)
        ada_pp(b_norm_g[l], ada_bc)
        sch.add("sp", lambda e: e.dma_start(out=gate_d[:, :], in_=ada_bc[:, 2048:3072]), reads=["ada_bc"],
                writes=["gate_d"], chan=uchan())
        sch.add("sp", lambda e: e.dma_start(out=qg_pp, in_=b_qg[l]), writes=["qg_pp"], chan=uchan())
        sch.add("dve", lambda e: e.tensor_scalar(out=qg_pp, in0=qg_pp, scalar1=float(128 ** -0.5), scalar2=None, op0=ALU.mult),
                reads=["qg_pp"], writes=["qg_pp"])
        ada_last = [sch.chan_last[("aw", b_, h_)] for b_ in range(2) for h_ in range(2)]
        wv = b_w_in[l].rearrange("(c p) n -> p c n", p=128)
        for cg in range(4):
            for kc in range(8):
                sch.add("pool", lambda e, kc=kc, cg=cg: e.dma_start(
                    out=Wq[:, kc, 1024 * cg:1024 * (cg + 1)], in_=wv[:, kc, 1024 * cg:1024 * (cg + 1)]),
                    writes=[("Wq", cg, kc)], chan="wq%d" % cg, extra_deps=ada_last)
        barrier(keep=("Wq",), skip_chans=("wq0", "wq1", "wq2", "wq3"))
        ar.reset(WORK)
        xb = [ar.take(F32, [128, 1024]) for _ in range(2)]
        xn = ar.take(BF16, [128, 1024])
        NHT = int(_os.environ.get("MK_NHT", "2"))
        hTs = [ar.take(BF16, [128, 8, STA]) for _ in range(NHT)]
        NKB = int(_os.environ.get("MK_NKB", "2"))
        sq = [ar.take(BF16, [128, STA]) for _ in range(NKB)]
        rs = [ar.take(F32, [128, STA]) for _ in range(NKB)]
        NQS = 4
        qst = [ar.take(BF16, [128, 2, STW]) for _ in range(NQS)]
        szst = ar.take(BF16, [128, 8, STA])
        qraw = [ar.take(F32, [128, STA]) for _ in range(NKB)] if QRAW else [None] * NKB
        if _os.environ.get("MK_DEBUG"):
            print("pass A arena used", ar.off, "of", ARENA_BYTES, flush=True)
        cnt = [0]
        qc = [0]
        fb = [0]
        for m in range(NSTA):
            hT = hTs[m % NHT]
            hb = m % NHT
            for tt in range(4):
                t = 4 * m + tt
                b = t % 2
                front(out_d, t, xb[b], xn, hT[:, :, 128 * tt:128 * (tt + 1)], 6 + (fb[0] % 2), ("xb", b), "xn", ("hT", hb, tt))
                fb[0] += 1
            hTr = [("hT", hb, i_) for i_ in range(4)]
            for h in range(24):
                k = cnt[0] % NKB
                pb = cnt[0] % 4
                pb2 = 4 + (cnt[0] % 2)
                cnt[0] += 1
                d = DIL[h // 8]

                def mm(e, h=h, pb=pb, hT=hT):
                    for kc in range(8):
                        ins = e.matmul(bank(pb), lhsT=Wq[:, kc, 128 * h:128 * (h + 1)], rhs=hT[:, kc, :],
                                       start=(kc == 0), stop=(kc == 7))
                    return ins
                sch.add("pe", mm, reads=hTr + [("Wq", h // 8, kc_) for kc_ in range(8)], writes=[("ps", pb)])
                qi = qc[0] % NQS
                qc[0] += 1
                pieces = [(qst[qi][:, hf, :].rearrange("p (r q) -> p r q", r=d), STW * hf, STW, d) for hf in range(2)]
                norm_to(pb, pb2, sq[k], rs[k], qg_pp, pieces, "k%d" % k, qraw=qraw[k], dst_res=("qst", qi))
                sch.add("sp", lambda e, h=h, m=m, qi=qi: e.dma_start(
                    out=qs_d[h, 2 * m:2 * m + 2].rearrange("a p q -> p a q"), in_=qst[qi]),
                    reads=[("qst", qi)], writes=[("qs_d", h, m)], chan=("qst", qi))
            for a in range(8):
                pb = 6 + (fb[0] % 2)
                fb[0] += 1

                def mz(e, a=a, pb=pb, hT=hT):
                    for kc in range(8):
                        ins = e.matmul(bank(pb), lhsT=Wq[:, kc, 3072 + 128 * a:3072 + 128 * (a + 1)], rhs=hT[:, kc, :],
                                       start=(kc == 0), stop=(kc == 7))
                    return ins
                sch.add("pe", mz, reads=hTr + [("Wq", 3, kc_) for kc_ in range(8)], writes=[("ps", pb)])
                sch.add("act", lambda e, a=a, pb=pb: e.activation(out=szst[:, a, :], in_=bank(pb), func=AF.Silu),
                        reads=[("ps", pb)], writes=[("szst", a)])
            for hf in range(2):
                sch.add("sp", lambda e, m=m, hf=hf: e.dma_start(out=szs_d[2 * m + hf], in_=szst[:, :, STW * hf:STW * (hf + 1)]),
                        reads=[("szst", a) for a in range(8)], writes=[("szst", a) for a in range(8)], chan=("szst", hf))

        barrier()
        ar.reset(kvs["END"])
        Wo = ar.take(BF16, [128, 8, 1024])
        etab = ar.take(BF16, [128, 12, 512])
        for h in range(3):
            sch.add("pool", lambda e, h=h: e.dma_start(out=etab[:, 4 * h:4 * h + 4, :], in_=etab_d[:, 4 * h:4 * h + 4, :]),
                    writes=[("etab", h)], chan="petab")
        wo = b_w_out[l].rearrange("(c p) n -> p c n", p=128)
        for a in range(0, 8, 2):
            sch.add("pool", lambda e, a=a: e.dma_start(out=Wo[:, a:a + 2, :], in_=wo[:, a:a + 2, :]),
                    writes=[("Wo", a)], chan="win")
        gate_bc = ar.take(F32, [128, 1024])
        sch.add("sp", lambda e: e.dma_start(out=gate_bc, in_=gate_d[:, :]), writes=["gate_bc"], chan=uchan())
        for a in range(8):
            sch.add("dve", lambda e, a=a: e.tensor_tensor(out=Wo[:, a, :], in0=Wo[:, a, :], in1=gate_bc, op=ALU.mult),
                    reads=[("Wo", 0), ("Wo", 2), ("Wo", 4), ("Wo", 6), "gate_bc"], writes=[("WoF", a)])
        barrier()
        NQT = 3
        NPT = int(_os.environ.get("MK_NPT", "8"))
        QT = [ar.take(BF16, [128, 4, STW]) for _ in range(NQT)]
        PT = [ar.take(BF16, [128, 512]) for _ in range(NPT)]
        accO2 = [ar.take(F32, [128, 4, STW]) for _ in range(2)]
        accL2 = [ar.take(F32, [128, 4, STW]) for _ in range(2)]
        rl = ar.take(F32, [128, 4, STW])
        szb = [ar.take(BF16, [128, 8, STW]) for _ in range(2)]
        yT = [ar.take(BF16, [128, 8, STW]) for _ in range(2)]
        xb = [ar.take(F32, [128, 1024]) for _ in range(2)]
        qcnt = [0]
        pcnt = [0]
        scnt = [0]
        ocnt = [0]
        pending = []
        DEFER_G = int(_os.environ.get("MK_DEFER_G", "2"))
        EM_MOD = int(_os.environ.get("MK_EM_MOD", "3"))
        EM_POOL = int(_os.environ.get("MK_EM_POOL", "2"))
        for m in range(NST):
            zb = m % 2
            sch.add("sp", lambda e, m=m, zb=zb: e.dma_start(out=szb[zb], in_=szs_d[m]),
                    writes=[("szb", zb)], chan=("szb", zb))
            yTm = yT[m % 2]
            yTn = ("yT", m % 2)
            for kh in range(2):
                accO = accO2[kh]
                accL = accL2[kh]
                nO = ("accO", kh)
                nL = ("accL", kh)
                for g in range(3):
                    if kh == 0 and g == DEFER_G and pending:
                        pending.pop(0)()
                    d = DIL[g]
                    j = 2 * g + kh
                    qb = qcnt[0] % NQT
                    qcnt[0] += 1
                    h0 = 8 * g + 4 * kh
                    sch.add("sp", lambda e, qb=qb, h0=h0, m=m: e.dma_start(
                        out=QT[qb], in_=qs_d[h0:h0 + 4, m].rearrange("a p q -> p a q")),
                        writes=[("QT", qb)], chan=("QT", qb))
                    nqu = min(128, STW // d)
                    U = 512 // (4 * nqu)
                    for pk in range(2):
                        if d == 1:
                            units = [(0, 2 * m + pk, 128 * pk)]
                            i0 = 0
                        else:
                            n = (STW * m // d) // 128
                            i0 = (STW * m // d) % 128
                            units = [(r, n, r * nqu) for r in range(pk * U, (pk + 1) * U)]
                        n_blk = units[0][1]
                        chunks = [1] if n_blk == 0 else [0, 1]
                        pts = {}
                        for c in chunks:
                            sb = scnt[0] % 4
                            scnt[0] += 1
                            pt_i = pcnt[0] % NPT
                            pcnt[0] += 1
                            pts[c] = pt_i

                            def qk(e, units=units, c=c, sb=sb, j=j, qb=qb, nqu=nqu, d=d):
                                for ui, (r, n, col) in enumerate(units):
                                    kb = r * (32 // d) + n - 1 + c
                                    ins = e.matmul(
                                        bank(sb)[:, 4 * nqu * ui:4 * nqu * (ui + 1)].rearrange("p (a q) -> p a q", a=4),
                                        lhsT=KT[:, j, 128 * kb:128 * (kb + 1)], rhs=QT[qb][:, :, col:col + nqu],
                                        start=True, stop=True)
                                return ins
                            sch.add("pe", qk, reads=[("QT", qb)], writes=[("ps", sb)])
                            sch.add("act", lambda e, sb=sb, pt_i=pt_i: e.activation(
                                out=PT[pt_i], in_=bank(sb), func=AF.Exp, bias=nshift_col, scale=1.0),
                                reads=[("ps", sb), "nshift_col"], writes=[("PT", pt_i)])
                            ev = etab[:, (g * 2 + kh) * 2 + c, :].rearrange("p (a i) -> p a i", a=4)[:, :, i0:i0 + nqu]
                            ev = ev.unsqueeze(1).to_broadcast([128, U, 4, nqu])
                            ptv = PT[pt_i].rearrange("p (u a q) -> p u a q", u=U, a=4)
                            sch.add("pool" if (pcnt[0] % EM_MOD) < EM_POOL else "dve", lambda e, ptv=ptv, ev=ev: e.tensor_tensor(
                                out=ptv, in0=ptv, in1=ev, op=ALU.mult),
                                reads=[("PT", pt_i)], writes=[("PT", pt_i)])
                        ob = 4 + (ocnt[0] % 2)
                        lb = 6 + (ocnt[0] % 2)
                        ocnt[0] += 1

                        def pv(e, units=units, chunks=chunks, pts=pts, ob=ob, j=j, nqu=nqu, d=d, ones=False):
                            for ui, (r, n, col) in enumerate(units):
                                for ci, c in enumerate(chunks):
                                    kb = r * (32 // d) + n - 1 + c
                                    lhs = ones_bf if ones else Vst[:, j, kb, :]
                                    ins = e.matmul(bank(ob)[:, 4 * nqu * ui:4 * nqu * (ui + 1)], lhsT=lhs,
                                                   rhs=PT[pts[c]][:, 4 * nqu * ui:4 * nqu * (ui + 1)],
                                                   start=(ci == 0), stop=(ci == len(chunks) - 1))
                            return ins
                        rds = [("PT", pts[c]) for c in chunks]
                        sch.add("pe", pv, reads=rds, writes=[("ps", ob)])
                        sch.add("pe", lambda e, pv=pv, lb=lb: pv(e, ob=lb, ones=True), reads=rds, writes=[("ps", lb)])
                        for (acc, bnk, nm) in ((accO, ob, nO), (accL, lb, nL)):
                            if d == 1:
                                av = acc[:, :, 128 * pk:128 * (pk + 1)].unsqueeze(1)
                            else:
                                av = acc.rearrange("p a (q r) -> p r a q", r=d)[:, pk * U:(pk + 1) * U, :, :]
                            bv = bank(bnk).rearrange("p (u a q) -> p u a q", u=U, a=4)
                            if g == 0:
                                sch.add("act", lambda e, av=av, bv=bv: e.activation(out=av, in_=bv, func=AF.Copy),
                                        reads=[("ps", bnk)], writes=[nm])
                            else:
                                sch.add("dve", lambda e, av=av, bv=bv: e.tensor_tensor(out=av, in0=bv, in1=av, op=ALU.add),
                                        reads=[("ps", bnk), nm], writes=[nm])
                sch.add("act", lambda e, accL=accL: e.activation(out=rl, in_=accL, func=AF.Ln), reads=[nL], writes=["rl"])
                sch.add("act", lambda e: e.activation(out=rl, in_=rl, func=AF.Exp, scale=-1.0), reads=["rl"], writes=["rl"])
                sch.add("dve", lambda e, accO=accO: e.tensor_tensor(out=accO, in0=accO, in1=rl, op=ALU.mult),
                        reads=[nO, "rl"], writes=[nO])
                sch.add("pool", lambda e, kh=kh, zb=zb, yTm=yTm, accO=accO: e.tensor_tensor(
                    out=yTm[:, 4 * kh:4 * kh + 4, :], in0=accO, in1=szb[zb][:, 4 * kh:4 * kh + 4, :], op=ALU.mult),
                    reads=[nO, ("szb", zb)], writes=[(yTn, kh)])
            def o_stage(m=m, yTm=yTm, yTn=yTn):
                for tt in range(2):
                    t = 2 * m + tt
                    b = t % 2
                    sch.add("sp", lambda e, t=t, b=b: e.dma_start(out=xb[b], in_=out_d[128 * t:128 * (t + 1), :]),
                            writes=[("xb", b)], chan=("xb", b))
                    for h in range(2):
                        sb = scnt[0] % 4
                        scnt[0] += 1

                        def mo(e, h=h, sb=sb, tt=tt, yTm=yTm):
                            for a in range(8):
                                ins = e.matmul(bank(sb), lhsT=yTm[:, a, 128 * tt:128 * (tt + 1)], rhs=Wo[:, a, 512 * h:512 * (h + 1)],
                                               start=(a == 0), stop=(a == 7))
                            return ins
                        sch.add("pe", mo, reads=[(yTn, 0), (yTn, 1)], writes=[("ps", sb)])
                        sch.add("dve", lambda e, h=h, sb=sb, b=b: e.tensor_tensor(
                            out=xb[b][:, 512 * h:512 * (h + 1)], in0=bank(sb), in1=xb[b][:, 512 * h:512 * (h + 1)], op=ALU.add),
                            reads=[("ps", sb), ("xb", b)], writes=[("xb", b)])
                    sch.add("sp", lambda e, t=t, b=b: e.dma_start(out=out_d[128 * t:128 * (t + 1), :], in_=xb[b]),
                            reads=[("xb", b)], writes=[("outrow", t)], chan=("st", b))
            pending.append(o_stage)
        while pending:
            pending.pop(0)()

    if do_kv:
        kv_phase()
    for l in range(n_b):
        attn_layer(l)

    sch.finish()

    sch.resolve()
    if _os.environ.get("MK_DEBUG"):
        print("sim_end_us", sch.sim_end, "n_ops", len(sch.ops), flush=True)
        sch.report()
        if globals().get("_DUMP"):
            lo, hi = globals()["_DUMP"]
            for i in sch.order:
                op = sch.ops[i]
                st = sch.sim_start[i]
                if lo <= st <= hi and op["fn"] is not None:
                    print("%9.2f %6.2f %-5s %s" % (st, sch.sim_endt[i] - st, op["issuer"], op["tag"]))
    sems = {}
    for k in sch.keys:
        nm = "s_" + (k if isinstance(k, str) else "_".join(str(z) for z in (k[1] if isinstance(k[1], tuple) else (k[1],))))
        sems[k] = es.enter_context(nc.semaphore(nm))
    block = es.enter_context(nc.Block())

    @block.tensor
    def _(e):
        sch.emit("pe", e, sems)

    @block.scalar
    def _(e):
        sch.emit("act", e, sems)

    @block.vector
    def _(e):
        sch.emit("dve", e, sems)

    @block.gpsimd
    def _(e):
        sch.emit("pool", e, sems)

    @block.sync
    def _(e):
        sch.emit("sp", e, sems)

    es.close()
    return nc


def _pp(v, n):
    return np.ascontiguousarray(np.asarray(v, np.float32).reshape(n, 128).T)


def _bc(v):
    v = np.asarray(v, np.float32).reshape(1, -1)
    return np.ascontiguousarray(np.broadcast_to(v, (128, v.shape[1])))


def _etab():
    H = 24
    slopes = np.exp2(-8.0 * np.arange(1, H + 1, dtype=np.float64) / H)
    dil = (1, 4, 16)
    j = np.arange(128)[:, None]
    i = np.arange(128)[None, :]
    E = np.zeros((128, 12, 512), np.float64)
    for g in range(3):
        for kh in range(2):
            for c in range(2):
                dq = (128 + i - j) if c == 0 else (i - j)
                valid = (dq >= 0) & (dq <= 128)
                for a in range(4):
                    h = 8 * g + 4 * kh + a
                    v = np.where(valid, np.exp(-slopes[h] * dil[g] * dq), 0.0)
                    E[:, (g * 2 + kh) * 2 + c, a * 128:(a + 1) * 128] = v
    return E.astype(np.float32)


def make_in_maps(inp):
    f = lambda k: np.asarray(inp[k], np.float32)
    shared = {
        "ident": np.eye(128, dtype=np.float32),
        "triu": np.triu(np.ones((128, 128), np.float32)),
        "etab": _etab(),
        "a_ada_w": f("a_ada_w"),
        "a_ada_b_bc": np.stack([_bc(f("a_ada_b")[l]) for l in range(2)]),
        "a_norm_g_pp": np.stack([_pp(f("a_norm_g")[l], 8) for l in range(2)]),
        "a_w_in": f("a_w_in"),
        "a_sgu_bc": np.stack([_bc(f("a_sgu_g")[l]) for l in range(2)]),
        "a_wsT": np.ascontiguousarray(np.transpose(f("a_w_spatial"), (0, 3, 1, 2))),
        "a_bsb": np.stack([np.ascontiguousarray(np.broadcast_to(f("a_b_spatial")[l][None], (128, 8, 128)))
                           for l in range(2)]),
        "a_w_out": f("a_w_out"),
        "kv_ada_w": f("kv_ada_w"),
        "kv_ada_b_bc": _bc(f("kv_ada_b")),
        "kv_norm_g_pp": _pp(f("kv_norm_g"), 8),
        "w_kv": f("w_kv"),
        "k_norm_g_pp": _pp(f("k_norm_g"), 1),
        "b_ada_w": f("b_ada_w"),
        "b_ada_b_bc": np.stack([_bc(f("b_ada_b")[l]) for l in range(2)]),
        "b_norm_g_pp": np.stack([_pp(f("b_norm_g")[l], 8) for l in range(2)]),
        "b_w_in": f("b_w_in"),
        "b_q_norm_g_pp": np.stack([_pp(f("b_q_norm_g")[l], 1) for l in range(2)]),
        "b_w_out": f("b_w_out"),
    }
    x = f("x")
    c = f("c")
    maps = []
    for b in range(8):
        m = dict(shared)
        m["x"] = np.ascontiguousarray(x[b])
        m["cT"] = _pp(c[b], 8)
        maps.append(m)
    return maps


_NC_CACHE = {}


def kernel(**inputs):
    key = "full"
    if key not in _NC_CACHE:
        _NC_CACHE[key] = build_program()
    nc = _NC_CACHE[key]
    in_maps = make_in_maps(inputs)
    res = run_bass_kernel_spmd(nc, in_maps, core_ids=list(range(8)))
    return np.stack([np.asarray(r["out"], np.float32) for r in res.results], axis=0)
```
